# Optimizing a Trainium2 kernel written in Bass

```python
import jax, jax.numpy as jnp
from jax import lax
import numpy as np

D_MODEL = 1024
BATCH = 4
SEQ = 8192
DEPTH = 1
DEC_BATCH = 32
DEC_SEQ = 16
PAST_LEN = 2048

CHUNK = 64
SGU_CHUNK = 128
SB_BLOCK = 128
MIX_WIDTH = D_MODEL
SGU_WIDTH = MIX_WIDTH // 2
SGU_HEADS = 4
SGU_HEAD_DIM = SGU_WIDTH // SGU_HEADS
SB_WIDTH = MIX_WIDTH - SGU_WIDTH
SB_HEADS = 8
SB_HEAD_DIM = SB_WIDTH // SB_HEADS
D_FF = ((8 * D_MODEL // 3 + 127) // 128) * 128
CONV_WIDTH = 3
IN_WIDTH = 2 * SGU_WIDTH + 3 * SB_WIDTH
EPS = 1e-6

kernel_name = "hybrid_sgu_stickbreak_convffn_step"


def _rms_norm(x, g):
    xf = x.astype(jnp.float32)
    y = xf * lax.rsqrt(jnp.mean(xf * xf, axis=-1, keepdims=True) + EPS)
    return (y * g.astype(jnp.float32)).astype(x.dtype)


def _layer_norm(x, g, b):
    xf = x.astype(jnp.float32)
    mu = jnp.mean(xf, axis=-1, keepdims=True)
    xc = xf - mu
    y = xc * lax.rsqrt(jnp.mean(xc * xc, axis=-1, keepdims=True) + EPS)
    return (y * g.astype(jnp.float32) + b.astype(jnp.float32)).astype(x.dtype)


def _spatial_gating(u, v, w_s, b_s):
    bsz, t, _ = v.shape
    L = min(t, SGU_CHUNK)
    nc = t // L
    mask = jnp.tril(jnp.ones((L, L), dtype=bool))
    w = jnp.where(mask[None], w_s[:, :L, :L], 0).astype(v.dtype)
    vh = v.reshape(bsz, nc, L, SGU_HEADS, SGU_HEAD_DIM)
    bias = b_s[:, :L].T.astype(v.dtype)[None, None, :, :, None]
    mixed = jnp.einsum('hts,bcshe->bcthe', w, vh) + bias
    return u * mixed.reshape(bsz, t, SGU_WIDTH)


def _stick_breaking_block(q, k, v, q_pos, k_pos):
    z = jnp.einsum('bqhd,bkhd->bhqk', q, k).astype(jnp.float32) * (SB_HEAD_DIM ** -0.5)
    reach = k_pos[None, :] < q_pos[:, None]
    log_fail = jnp.where(reach, jax.nn.log_sigmoid(-z), 0.0)
    incl = lax.cumsum(log_fail, axis=3, reverse=True)
    excl = jnp.concatenate([incl[..., 1:], jnp.zeros_like(incl[..., :1])], axis=-1)
    w = jnp.where(reach, jnp.exp(jax.nn.log_sigmoid(z) + excl), 0.0)
    return jnp.einsum('bhqk,bkhd->bqhd', w.astype(v.dtype), v)


def _stick_breaking(q, k, v, q_pos, k_pos):
    bsz, t, h, d = q.shape
    if t <= SB_BLOCK:
        return _stick_breaking_block(q, k, v, q_pos, k_pos)
    nb = t // SB_BLOCK
    qb = jnp.moveaxis(q.reshape(bsz, nb, SB_BLOCK, h, d), 1, 0)
    pb = q_pos.reshape(nb, SB_BLOCK)
    ob = lax.map(lambda a: _stick_breaking_block(a[0], k, v, a[1], k_pos), (qb, pb))
    return jnp.moveaxis(ob, 0, 1).reshape(bsz, t, h, d)


def _causal_dwconv(h, hist, w, b):
    t = h.shape[1]
    full = jnp.concatenate([hist, h], axis=1)
    y = full[:, 0:t] * w[0] + full[:, 1:t + 1] * w[1] + full[:, 2:t + 2] * w[2] + b
    return y, full[:, -(CONV_WIDTH - 1):]


def _layer(x, past_k, past_v, conv_hist, w_in, g_pre_mix, ln_v_g, ln_v_b, w_spatial, b_spatial,
           g_out_a, g_out_b, w_out, g_post_mix, g_pre_ffn, w_up, conv_w, conv_b, w_down, g_post_ffn):
    bsz, t, _ = x.shape
    past = past_k.shape[1]
    h = _rms_norm(x, g_pre_mix)
    p = h @ w_in
    za = jax.nn.gelu(p[..., :2 * SGU_WIDTH])
    o = 2 * SGU_WIDTH
    q = p[..., o:o + SB_WIDTH].reshape(bsz, t, SB_HEADS, SB_HEAD_DIM)
    k = p[..., o + SB_WIDTH:o + 2 * SB_WIDTH].reshape(bsz, t, SB_HEADS, SB_HEAD_DIM)
    v = p[..., o + 2 * SB_WIDTH:o + 3 * SB_WIDTH].reshape(bsz, t, SB_HEADS, SB_HEAD_DIM)
    u_a = za[..., :SGU_WIDTH]
    v_a = _layer_norm(za[..., SGU_WIDTH:], ln_v_g, ln_v_b)
    out_a = _spatial_gating(u_a, v_a, w_spatial, b_spatial)
    k_all = jnp.concatenate([past_k, k], axis=1)
    v_all = jnp.concatenate([past_v, v], axis=1)
    q_pos = past + jnp.arange(t)
    k_pos = jnp.arange(past + t)
    out_b = _stick_breaking(q, k_all, v_all, q_pos, k_pos).reshape(bsz, t, SB_WIDTH)
    mix = jnp.concatenate([_rms_norm(out_a, g_out_a), _rms_norm(out_b, g_out_b)], axis=-1) @ w_out
    x = x + _rms_norm(mix, g_post_mix)
    up = _rms_norm(x, g_pre_ffn) @ w_up
    up_c, new_hist = _causal_dwconv(up, conv_hist, conv_w, conv_b)
    f = (jax.nn.gelu(up_c[..., :D_FF]) * up_c[..., D_FF:]) @ w_down
    x = x + _rms_norm(f, g_post_ffn)
    return x, k, v, v_a, new_hist


def setup_inputs(seed: int = 0) -> dict:
    key = jax.random.key(seed)
    ks = jax.random.split(key, 24)
    n = lambda i, shape, s=1.0: jax.random.normal(ks[i], shape, jnp.float32) * s
    gain = lambda i, width: 1.0 + n(i, (DEPTH, width), 0.1)
    return {
        "x_prompt": n(0, (BATCH, SEQ, D_MODEL)),
        "x_sample": n(1, (DEC_BATCH, DEC_SEQ, D_MODEL)),
        "cache_sb_k": n(2, (DEPTH, DEC_BATCH, PAST_LEN, SB_HEADS, SB_HEAD_DIM)),
        "cache_sb_v": n(3, (DEPTH, DEC_BATCH, PAST_LEN, SB_HEADS, SB_HEAD_DIM)),
        "cache_ffn_conv": n(4, (DEPTH, DEC_BATCH, CONV_WIDTH - 1, 2 * D_FF)),
        "w_in": n(5, (DEPTH, D_MODEL, IN_WIDTH), D_MODEL ** -0.5),
        "g_pre_mix": gain(6, D_MODEL),
        "ln_v_g": gain(7, SGU_WIDTH),
        "ln_v_b": n(8, (DEPTH, SGU_WIDTH), 0.01),
        "w_spatial": n(9, (DEPTH, SGU_HEADS, SGU_CHUNK, SGU_CHUNK), SGU_CHUNK ** -0.5),
        "b_spatial": 1.0 + n(10, (DEPTH, SGU_HEADS, SGU_CHUNK), 0.1),
        "g_out_a": gain(11, SGU_WIDTH),
        "g_out_b": gain(12, SB_WIDTH),
        "w_out": n(13, (DEPTH, MIX_WIDTH, D_MODEL), MIX_WIDTH ** -0.5),
        "g_post_mix": gain(14, D_MODEL),
        "g_pre_ffn": gain(15, D_MODEL),
        "w_up": n(16, (DEPTH, D_MODEL, 2 * D_FF), D_MODEL ** -0.5),
        "conv_w": n(17, (DEPTH, CONV_WIDTH, 2 * D_FF), CONV_WIDTH ** -0.5),
        "conv_b": n(18, (DEPTH, 2 * D_FF), 0.01),
        "w_down": n(19, (DEPTH, D_FF, D_MODEL), D_FF ** -0.5),
        "g_post_ffn": gain(20, D_MODEL),
    }


def reference(x_prompt, x_sample, cache_sb_k, cache_sb_v, cache_ffn_conv, w_in, g_pre_mix, ln_v_g,
              ln_v_b, w_spatial, b_spatial, g_out_a, g_out_b, w_out, g_post_mix, g_pre_ffn, w_up,
              conv_w, conv_b, w_down, g_post_ffn):
    yp, ys = x_prompt, x_sample
    bp = x_prompt.shape[0]
    empty_kv = jnp.zeros((bp, 0, SB_HEADS, SB_HEAD_DIM), x_prompt.dtype)
    zero_hist = jnp.zeros((bp, CONV_WIDTH - 1, 2 * D_FF), x_prompt.dtype)
    kp_l, vp_l, cp_l, ks_l, vs_l, vas_l, cs_l = [], [], [], [], [], [], []
    for l in range(DEPTH):
        params = (w_in[l], g_pre_mix[l], ln_v_g[l], ln_v_b[l], w_spatial[l], b_spatial[l],
                  g_out_a[l], g_out_b[l], w_out[l], g_post_mix[l], g_pre_ffn[l], w_up[l],
                  conv_w[l], conv_b[l], w_down[l], g_post_ffn[l])
        yp, kp, vp, _, cp = _layer(yp, empty_kv, empty_kv, zero_hist, *params)
        ys, ksm, vsm, vas, cs = _layer(ys, cache_sb_k[l], cache_sb_v[l], cache_ffn_conv[l], *params)
        kp_l.append(kp); vp_l.append(vp); cp_l.append(cp)
        ks_l.append(ksm); vs_l.append(vsm); vas_l.append(vas); cs_l.append(cs)
    return (yp, ys, jnp.stack(kp_l), jnp.stack(vp_l), jnp.stack(ks_l), jnp.stack(vs_l),
            jnp.stack(vas_l), jnp.stack(cp_l), jnp.stack(cs_l))
```

```python
import numpy as np
from contextlib import ExitStack
import concourse.bass as bass
import concourse.mybir as mybir
from concourse.bass_utils import run_bass_kernel_spmd

F32 = mybir.dt.float32
BF16 = mybir.dt.bfloat16
AF = mybir.ActivationFunctionType
ALU = mybir.AluOpType
ENGS = ("sp", "act", "pool", "dve", "pe")
SEM_EPOCH = 8000

NPOS = 64
HALO = 31
NOWN = 34
SAMP = 33
EPS = 1e-6
NEG = -30000.0
LEVEL = 9
AKINDS = ("key", "own", "samp")
ASTOP = 99
B0MODE = 9
SKIPPV = 0


class _Stop(Exception):
    pass


class Buf:
    __slots__ = ("name", "w", "r")

    def __init__(self, name=""):
        self.name = name
        self.w = None
        self.r = []


class Op:
    __slots__ = ("id", "eng", "fn", "deps", "dmakey", "sem", "val", "awaited")


class Prog:
    def __init__(self, nc, es):
        self.nc = nc
        self.es = es
        self.ops = []
        self.start = 0
        self.dmasem = {}
        self.nsem = 0

    def add(self, eng, fn, reads=(), writes=(), dmakey=None):
        op = Op()
        op.id = len(self.ops)
        op.eng = eng
        op.fn = fn
        deps = set()
        for b in reads:
            if b.w is not None:
                deps.add(b.w)
        for b in writes:
            if b.w is not None:
                deps.add(b.w)
            deps.update(b.r)
        op.deps = {d for d in deps if d >= self.start}
        op.dmakey = dmakey
        op.sem = None
        op.val = 0
        op.awaited = False
        for b in reads:
            b.r.append(op.id)
        for b in writes:
            b.w = op.id
            b.r = []
        self.ops.append(op)
        return op

    def flush(self):
        nc = self.nc
        ops = self.ops
        cur = ops[self.start:]
        if not cur:
            return
        for op in cur:
            for d in op.deps:
                dop = ops[d]
                if dop.eng == "pe" and op.eng == "pe" and dop.dmakey is None:
                    continue
                dop.awaited = True
        per = {e: [o for o in cur if o.eng == e] for e in ENGS}
        last = {}
        for e in ENGS:
            comp = [o for o in per[e] if o.dmakey is None and o.fn is not None]
            if comp:
                comp[-1].awaited = True
                last[e] = comp[-1]
        for e in ENGS:
            n = 0
            sem = None
            for op in per[e]:
                if op.fn is None:
                    continue
                if op.dmakey is not None:
                    if op.dmakey not in self.dmasem:
                        self.nsem += 1
                        self.dmasem[op.dmakey] = [self.es.enter_context(nc.semaphore("sd%d" % self.nsem)), 0]
                    ent = self.dmasem[op.dmakey]
                    ent[1] += 16
                    op.sem, op.val = ent[0], ent[1]
                elif op.awaited:
                    if sem is None or n >= SEM_EPOCH:
                        self.nsem += 1
                        sem = self.es.enter_context(nc.semaphore("se%d" % self.nsem))
                        n = 0
                    n += 1
                    op.sem, op.val = sem, n
        barrier = [(o.sem, o.val) for o in last.values()]
        barrier += [(s, v) for (s, v) in self.dmasem.values()]

        def run(ename, e):
            seen = {}

            def wait(sem, val):
                k = id(sem)
                if seen.get(k, 0) >= val:
                    return
                seen[k] = val
                e.wait_ge(sem, val)

            for op in per[ename]:
                need = {}
                for d in sorted(op.deps):
                    dop = ops[d]
                    if dop.sem is None:
                        continue
                    if dop.eng == "pe" and ename == "pe" and dop.dmakey is None:
                        continue
                    k = id(dop.sem)
                    if k not in need or need[k][1] < dop.val:
                        need[k] = (dop.sem, dop.val)
                for (s_, v_) in need.values():
                    wait(s_, v_)
                if op.fn is None:
                    continue
                ins = op.fn(e)
                if op.dmakey is not None:
                    ins.then_inc(op.sem, 16)
                elif op.awaited:
                    ins.then_inc(op.sem, 1)
            for (s, v) in barrier:
                wait(s, v)

        with nc.Block() as block:
            @block.sync
            def _(e):
                run("sp", e)

            @block.scalar
            def _(e):
                run("act", e)

            @block.gpsimd
            def _(e):
                run("pool", e)

            @block.vector
            def _(e):
                run("dve", e)

            @block.tensor
            def _(e):
                run("pe", e)
        self.start = len(ops)


def build():
    nc = bass.Bass("TRN2", target_bir_lowering=False)
    di = lambda n, s: nc.dram_tensor(n, list(s), F32, kind="ExternalInput").ap()
    do = lambda n, s: nc.dram_tensor(n, list(s), F32, kind="ExternalOutput").ap()
    dscr = lambda n, s, dt: nc.dram_tensor(n, list(s), dt, kind="Internal").ap()

    xk = di("xk", [NPOS * 128, 1024])
    xs = di("xs", [128, 1024])
    ck = di("ck", [4, 2048, 512])
    cv = di("cv", [4, 2048, 512])
    cconv = di("cconv", [8, 5632])
    w_in = di("w_in", [1024, 2560])
    w_out = di("w_out", [1024, 1024])
    w_up = di("w_up", [1024, 5632])
    w_down = di("w_down", [2816, 1024])
    g_pre_mix = di("g_pre_mix", [1024])
    ln_v_g = di("ln_v_g", [512])
    ln_v_b = di("ln_v_b", [512])
    w_spatial = di("w_spatial", [4, 128, 128])
    b_spatial = di("b_spatial", [4, 128])
    g_out_a = di("g_out_a", [512])
    g_out_b = di("g_out_b", [512])
    g_post_mix = di("g_post_mix", [1024])
    g_pre_ffn = di("g_pre_ffn", [1024])
    conv_w = di("conv_w", [3, 5632])
    conv_b = di("conv_b", [1, 5632])
    g_post_ffn = di("g_post_ffn", [1024])

    NPR = NOWN - 2
    y_o = do("y", [NPR * 128, 1024])
    ys_o = do("ys", [128, 1024])
    ko_o = do("ko", [NPR * 128, 512])
    vo_o = do("vo", [NPR * 128, 512])
    kso_o = do("kso", [128, 512])
    vso_o = do("vso", [128, 512])
    vas_o = do("vas", [128, 512])
    upl_o = do("upl", [128, 5632])
    ups_o = do("ups", [128, 5632])

    kts = dscr("kts", [4, 128, NPOS * 128], BF16)
    vss = dscr("vss", [4, 128, NPOS, 128], BF16)
    x1s = dscr("x1s", [NOWN, 128, 1024], F32)
    h2s = dscr("h2s", [8, 128, NOWN * 128], BF16)
    wups = dscr("wups", [8, 128, 5632], BF16)
    nas = dscr("nas", [4, 128, NOWN * 128], BF16)

    with ExitStack() as es0:
      es = es0.enter_context(ExitStack())
      try:
        P = Prog(nc, es)
        add = P.add
        OUT = Buf("outs")

        def sb(st, n, s, d):
            return st.enter_context(nc.sbuf_tensor(n, list(s), d))

        def ps(st, n, s, d=F32):
            return st.enter_context(nc.psum_tensor(n, list(s), d))

        def dma(q, out, in_, reads, writes, key):
            add(q, lambda e: e.dma_start(out=out, in_=in_), reads, writes, dmakey=key)

        def act(out, in_, func, reads, writes, **kw):
            add("act", lambda e: e.activation(out=out, in_=in_, func=func, **kw), reads, writes)

        def mm(out, lhsT, rhs, start, stop, reads, writes, tile_position=None):
            if tile_position is None:
                add("pe", lambda e: e.matmul(out, lhsT=lhsT, rhs=rhs, start=start, stop=stop, skip_group_check=True), reads, writes)
            else:
                add("pe", lambda e: e.matmul(out, lhsT=lhsT, rhs=rhs, start=start, stop=stop, skip_group_check=True, tile_position=tile_position), reads, writes)

        def tr(out, in_, ident, reads, writes):
            add("pe", lambda e: e.transpose(out=out, in_=in_, identity=ident), reads, writes)

        def stt(eng, out, in0, scalar, in1, op0, op1, reads, writes):
            add(eng, lambda e: e.scalar_tensor_tensor(out=out, in0=in0, scalar=scalar, in1=in1, op0=op0, op1=op1), reads, writes)

        def ts(eng, out, in0, s1, s2, op0, op1, reads, writes):
            if s2 is None:
                add(eng, lambda e: e.tensor_scalar(out=out, in0=in0, scalar1=s1, scalar2=None, op0=op0), reads, writes)
            else:
                add(eng, lambda e: e.tensor_scalar(out=out, in0=in0, scalar1=s1, scalar2=s2, op0=op0, op1=op1), reads, writes)

        def tt(eng, out, in0, in1, op, reads, writes):
            add(eng, lambda e: e.tensor_tensor(out=out, in0=in0, in1=in1, op=op), reads, writes)

        def cp(eng, out, in_, reads, writes):
            add(eng, lambda e: e.tensor_copy(out=out, in_=in_), reads, writes)

        def rstd(r, ssum, n, b):
            ts("dve", r, ssum, 1.0 / n, EPS, ALU.mult, ALU.add, [b], [b])
            act(r, r, AF.Sqrt, [b], [b])
            add("dve", lambda e: e.reciprocal(out=r, in_=r), [b], [b])

        ident = sb(es, "ident", [128, 128], BF16)
        identf = sb(es, "identf", [128, 128], F32)
        negT = sb(es, "negT", [128, 128], BF16)
        negO = sb(es, "negO", [128, 128], BF16)
        maskb = sb(es, "maskb", [128, 128], BF16)
        masks = sb(es, "masks", [128, 4, 8, 32], BF16)
        onesf = sb(es, "onesf", [128, 2], BF16)
        tmpf = sb(es, "tmpf", [128, 128], F32)
        gpre = sb(es, "gpre", [128, 1024], F32)
        lng = sb(es, "lng", [128, 512], F32)
        lnb = sb(es, "lnb", [128, 512], F32)
        goa = sb(es, "goa", [128, 512], F32)
        gpost = sb(es, "gpost", [128, 1024], F32)
        gpf = sb(es, "gpf", [128, 1024], F32)
        gpostf = sb(es, "gpostf", [128, 1024], F32)
        gob = sb(es, "gob", [128, 4], F32)
        bsp = sb(es, "bsp", [128, 4], F32)
        bsps = sb(es, "bsps", [128, 4], F32)
        wsT = sb(es, "wsT", [128, 4, 128], BF16)
        wsTs = sb(es, "wsTs", [128, 4, 128], BF16)
        cvec = sb(es, "cvec", [128, 44, 12], F32)
        hist = sb(es, "hist", [128, 44, 2], F32)
        ssb = sb(es, "ssb", [128, NOWN], F32)
        CONST = Buf("const")

        with ExitStack() as ph:
            wn = sb(ph, "wn", [128, 4, 128], F32)
            wnb = sb(ph, "wnb", [128, 4, 128], BF16)
            cst = sb(ph, "cst", [12, 5632], F32)
            pst = ps(ph, "pst", [128, 4, 128], BF16)
            pc0 = ps(ph, "pc0", [128, 22, 12], F32)
            pc1 = ps(ph, "pc1", [128, 22, 12], F32)
            C2 = Buf("c2")
            add("pool", lambda e: e.memset(identf[:], 0.0), [], [CONST])
            add("pool", lambda e: e.affine_select(out=identf[:], in_=identf[:], pattern=[[-1, 128]], compare_op=ALU.not_equal, fill=1.0, base=0, channel_multiplier=1), [CONST], [CONST])
            cp("dve", ident[:], identf[:], [CONST], [CONST])
            add("pool", lambda e: e.memset(tmpf[:], -1.0), [], [C2])
            cp("dve", negO[:], tmpf[:], [C2], [CONST])
            add("pool", lambda e: e.affine_select(out=tmpf[:], in_=tmpf[:], pattern=[[-1, 128]], compare_op=ALU.is_ge, fill=0.0, base=0, channel_multiplier=1), [C2], [C2])
            cp("dve", negT[:], tmpf[:], [C2], [CONST])
            add("pool", lambda e: e.memset(tmpf[:], 0.0), [C2], [C2])
            add("pool", lambda e: e.affine_select(out=tmpf[:], in_=tmpf[:], pattern=[[1, 128]], compare_op=ALU.is_gt, fill=NEG, base=0, channel_multiplier=-1), [C2], [C2])
            cp("dve", maskb[:], tmpf[:], [C2], [CONST])
            for bi in range(4):
                add("pool", lambda e: e.memset(tmpf[:, 0:32], 0.0), [C2], [C2])
                add("pool", lambda e, bi=bi: e.affine_select(out=tmpf[:, 0:32], in_=tmpf[:, 0:32], pattern=[[1, 32]], compare_op=ALU.is_gt, fill=NEG, base=32 * bi, channel_multiplier=-1), [C2], [C2])
                add("pool", lambda e, bi=bi: e.affine_select(out=tmpf[:, 0:32], in_=tmpf[:, 0:32], pattern=[[0, 32]], compare_op=ALU.is_ge, fill=NEG, base=-32 * bi, channel_multiplier=1), [C2], [C2])
                add("pool", lambda e, bi=bi: e.affine_select(out=tmpf[:, 0:32], in_=tmpf[:, 0:32], pattern=[[0, 32]], compare_op=ALU.is_gt, fill=NEG, base=32 * bi + 16, channel_multiplier=-1), [C2], [C2])
                for h in range(8):
                    cp("dve", masks[:, bi, h, :], tmpf[:, 0:32], [C2], [CONST])
            add("pool", lambda e: e.memset(onesf[:], 1.0), [], [CONST])
            add("pool", lambda e: e.memset(bsps[:], 0.0), [], [CONST])
            for i, (t_, src) in enumerate([(gpre, g_pre_mix), (lng, ln_v_g), (lnb, ln_v_b), (goa, g_out_a), (gpost, g_post_mix), (gpf, g_pre_ffn), (gpostf, g_post_ffn)]):
                dma("sp", t_[:], src.partition_broadcast(128), [], [CONST], "cl%d" % i)
            add("sp", lambda e: e.dma_start(out=gob[:], in_=g_out_b.rearrange("(c p) -> p c", p=128), allow_slow_non_contiguous=True), [], [CONST], dmakey="cl7")
            add("sp", lambda e: e.dma_start(out=bsp[:], in_=b_spatial.rearrange("h t -> t h"), allow_slow_non_contiguous=True), [], [CONST], dmakey="cl8")
            for i in range(4):
                add("sp", lambda e, i=i: e.dma_start(out=bsps[32 * i:32 * i + 16, :], in_=b_spatial[:, 0:16].rearrange("h t -> t h"), allow_slow_non_contiguous=True), [CONST], [CONST], dmakey="cl9")
            WN = Buf("wn")
            for variant in range(2):
                if variant == 0:
                    dma("sp", wn[:], w_spatial.rearrange("h t s -> t h s"), [WN], [WN], "wn")
                else:
                    add("pool", lambda e: e.memset(wn[:], 0.0), [WN], [WN])
                    for i in range(4):
                        dma("sp", wn[32 * i:32 * i + 16, :, 32 * i:32 * i + 16], w_spatial[:, 0:16, 0:16].rearrange("h t s -> t h s"), [WN], [WN], "wn")
                for h in range(4):
                    add("pool", lambda e, h=h: e.affine_select(out=wn[:, h, :], in_=wn[:, h, :], pattern=[[-1, 128]], compare_op=ALU.is_ge, fill=0.0, base=0, channel_multiplier=1), [WN], [WN])
                cp("dve", wnb[:], wn[:], [WN], [WN])
                for h in range(4):
                    tr(pst[:, h, :], wnb[:, h, :], ident[:], [WN, CONST], [WN])
                cp("dve", (wsT if variant == 0 else wsTs)[:], pst[:], [WN], [WN, CONST])
            CS = Buf("cst")
            dma("sp", cst[0:3, :], conv_w, [], [CS], "cs0")
            dma("sp", cst[3:4, :], conv_b, [], [CS], "cs1")
            dma("sp", cst[4:12, :], cconv, [], [CS], "cs2")
            for c in range(44):
                pc = pc0 if c < 22 else pc1
                tr(pc[:, c % 22, :], cst[:, c * 128:(c + 1) * 128], identf[0:12, 0:12], [CS, CONST], [CS])
            cp("dve", cvec[:, 0:22, :], pc0[:], [CS], [CONST])
            cp("dve", cvec[:, 22:44, :], pc1[:], [CS], [CONST])
            wtmp = [sb(ph, "wtmp%d" % i, [128, 5632], BF16) for i in range(2)]
            WT = [Buf() for _ in range(2)]
            for c in range(8):
                dma("pool", wtmp[c % 2][:], w_up[c * 128:(c + 1) * 128, :], [], [WT[c % 2]], "wtl%d" % (c % 2))
                dma("sp", wups[c], wtmp[c % 2][:], [WT[c % 2]], [], "wts%d" % (c % 2))
            P.flush()

        with ExitStack() as st_ab:
            obT = sb(st_ab, "obT", [128, 4, NOWN * 128], BF16)
            NAT = [Buf("naT%d" % i) for i in range(NOWN)]
            OBT = Buf("obT")
            SSB = Buf("ssb")
            with ExitStack() as st_q:
                qT = sb(st_q, "qT", [128, 4, NOWN * 128], BF16)
                QT = [Buf("qT%d" % i) for i in range(NOWN)]
                kTn = sb(st_q, "kTn", [128, 4, 128], BF16)
                vnew = sb(st_q, "vnew", [128, 512], BF16)
                KTN = Buf("kTn")
                VNEW = Buf("vnew")
                KTS = Buf("kts")
                VSS = Buf("vss")

                if LEVEL < 1:
                    return nc
                with ExitStack() as ph:
                    win = sb(ph, "win", [128, 8, 2560], BF16)
                    WIN = Buf("win")
                    for c in range(8):
                        dma("pool", win[:, c, :], w_in[c * 128:(c + 1) * 128, :], [], [WIN], "win")
                    xt = [sb(ph, "xt%d" % i, [128, 1024], F32) for i in range(2)]
                    XT = [Buf() for _ in range(2)]
                    junk = sb(ph, "junk", [128, 1024], BF16)
                    JK = Buf()
                    st = sb(ph, "stat", [128, 16], F32)
                    ST = Buf()
                    hn = sb(ph, "hn", [128, 1024], BF16)
                    HN = Buf()
                    xT = [sb(ph, "xT%d" % i, [128, 8, 128], BF16) for i in range(2)]
                    XTT = [Buf() for _ in range(2)]
                    kst = [sb(ph, "kst%d" % i, [128, 4, 128], BF16) for i in range(2)]
                    KST = [Buf() for _ in range(2)]
                    vst = [sb(ph, "vst%d" % i, [128, 512], BF16) for i in range(2)]
                    VST = [Buf() for _ in range(2)]
                    vof = [sb(ph, "vof%d" % i, [128, 512], F32) for i in range(2)]
                    VOF = [Buf() for _ in range(2)]
                    kof = [sb(ph, "kof%d" % i, [128, 512], F32) for i in range(2)]
                    KOF = [Buf() for _ in range(2)]
                    za = sb(ph, "za", [128, 1024], F32)
                    ZA = Buf()
                    vn = sb(ph, "vn", [128, 512], F32)
                    VN = Buf()
                    va = [sb(ph, "va%d" % i, [128, 512], F32) for i in range(2)]
                    VA = [Buf() for _ in range(2)]
                    vab = sb(ph, "vab", [128, 512], BF16)
                    VAB = Buf()
                    outa = sb(ph, "outa", [128, 512], F32)
                    OA = Buf()
                    nab = sb(ph, "nab", [128, 512], BF16)
                    NAB = Buf()
                    nast = [sb(ph, "nast%d" % i, [128, 4, 128], BF16) for i in range(2)]
                    NAST = [Buf() for _ in range(2)]
                    ptr = ps(ph, "ptr", [128, 8, 128], BF16)
                    PTR = Buf()
                    pk = ps(ph, "pk", [128, 512])
                    PK = Buf()
                    pv = ps(ph, "pv", [128, 512])
                    PV = Buf()
                    pk2 = ps(ph, "pk2", [128, 512])
                    PK2 = Buf()
                    pq = ps(ph, "pq", [128, 512])
                    PQ = Buf()
                    puv = ps(ph, "puv", [128, 1024])
                    PUV = Buf()
                    psg = ps(ph, "psg", [128, 512])
                    PSG = Buf()

                    tiles = [("key", p, None) for p in range(HALO)]
                    tiles += [("own", p, p - HALO) for p in range(HALO, NPOS)]
                    tiles += [("samp", None, SAMP)]
                    tiles = [t_ for t_ in tiles if t_[0] in AKINDS]
                    for ti, (kind, pos, oi) in enumerate(tiles):
                        s = ti % 2
                        src = xs if kind == "samp" else xk[pos * 128:(pos + 1) * 128, :]
                        dma("sp", xt[s][:], src, [], [XT[s]], "xt%d" % s)
                        act(junk[:], xt[s][:], AF.Square, [XT[s]], [ST], accum_out=st[:, 0:1])
                        rstd(st[:, 1:2], st[:, 0:1], 1024, ST)
                        stt("dve", hn[:], xt[s][:], st[:, 1:2], gpre[:], ALU.mult, ALU.mult, [XT[s], ST, CONST], [HN])
                        for c in range(8):
                            tr(ptr[:, c, :], hn[:, c * 128:(c + 1) * 128], ident[:], [HN, CONST], [PTR])
                        cp("dve", xT[s][:], ptr[:], [PTR], [XTT[s]])
                        for fc in range(4):
                            for c in range(8):
                                mm(pk[:, fc * 128:(fc + 1) * 128], win[:, c, 1536 + fc * 128:1536 + (fc + 1) * 128], xT[s][:, c, :],
                                   fc == 0 and c == 0, fc == 3 and c == 7, [WIN, XTT[s]], [PK])
                        for c in range(8):
                            mm(pv[:], xT[s][:, c, :], win[:, c, 2048:2560], c == 0, c == 7, [WIN, XTT[s]], [PV])
                        if kind == "samp":
                            act(kTn[:].rearrange("p c t -> p (c t)"), pk[:], AF.Copy, [PK], [KTN], scale=0.125)
                        else:
                            act(kst[s][:].rearrange("p c t -> p (c t)"), pk[:], AF.Copy, [PK], [KST[s]], scale=0.125)
                            dma("pool", kts.rearrange("c p t -> p c t")[:, :, pos * 128:(pos + 1) * 128], kst[s][:], [KST[s]], [], "kst%d" % s)
                        if kind == "key":
                            cp("dve", vst[s][:], pv[:], [PV], [VST[s]])
                        else:
                            act(vof[s][:], pv[:], AF.Copy, [PV], [VOF[s]])
                            cp("dve", vst[s][:], vof[s][:], [VOF[s]], [VST[s]])
                        if kind == "samp":
                            cp("dve", vnew[:], vof[s][:], [VOF[s]], [VNEW])
                        else:
                            dma("pool", vss.rearrange("c p n d -> p c n d")[:, :, pos, :], vst[s][:].rearrange("p (c d) -> p c d", d=128), [VST[s]], [], "vst%d" % s)
                        if kind == "key":
                            continue
                        if ASTOP < 1:
                            continue
                        for c in range(8):
                            mm(pk2[:], xT[s][:, c, :], win[:, c, 1536:2048], c == 0, c == 7, [WIN, XTT[s]], [PK2])
                        act(kof[s][:], pk2[:], AF.Copy, [PK2], [KOF[s]])
                        if kind == "samp":
                            dma("pool", vso_o, vof[s][:], [VOF[s]], [], "vof%d" % s)
                            dma("pool", kso_o, kof[s][:], [KOF[s]], [], "kof%d" % s)
                        elif oi >= 1:
                            dma("pool", vo_o[(oi - 1) * 128:oi * 128, :], vof[s][:], [VOF[s]], [], "vof%d" % s)
                            dma("pool", ko_o[(oi - 1) * 128:oi * 128, :], kof[s][:], [KOF[s]], [], "kof%d" % s)
                        if ASTOP < 2:
                            continue
                        for fc in range(4):
                            for c in range(8):
                                mm(pq[:, fc * 128:(fc + 1) * 128], win[:, c, 1024 + fc * 128:1024 + (fc + 1) * 128], xT[s][:, c, :],
                                   fc == 0 and c == 0, fc == 3 and c == 7, [WIN, XTT[s]], [PQ])
                        cp("dve", qT[:, :, oi * 128:(oi + 1) * 128], pq[:].rearrange("p (c t) -> p c t", t=128), [PQ], [QT[oi]])
                        if ASTOP < 3:
                            continue
                        for hf in range(2):
                            for c in range(8):
                                mm(puv[:, hf * 512:(hf + 1) * 512], xT[s][:, c, :], win[:, c, hf * 512:(hf + 1) * 512], c == 0, c == 7, [WIN, XTT[s]], [PUV])
                        act(za[:], puv[:], AF.Gelu_apprx_tanh, [PUV], [ZA])
                        if ASTOP < 4:
                            continue
                        act(junk[:, 0:512], za[:, 512:1024], AF.Identity, [ZA], [ST], accum_out=st[:, 2:3])
                        act(junk[:, 512:1024], za[:, 512:1024], AF.Square, [ZA], [ST], accum_out=st[:, 3:4])
                        ts("dve", st[:, 4:5], st[:, 2:3], 1.0 / 512, None, ALU.mult, ALU.bypass, [ST], [ST])
                        tt("dve", st[:, 5:6], st[:, 4:5], st[:, 4:5], ALU.mult, [ST], [ST])
                        stt("dve", st[:, 6:7], st[:, 3:4], 1.0 / 512, st[:, 5:6], ALU.mult, ALU.subtract, [ST], [ST])
                        ts("dve", st[:, 6:7], st[:, 6:7], 1.0, EPS, ALU.mult, ALU.add, [ST], [ST])
                        act(st[:, 6:7], st[:, 6:7], AF.Sqrt, [ST], [ST])
                        add("dve", lambda e: e.reciprocal(out=st[:, 7:8], in_=st[:, 6:7]), [ST], [ST])
                        stt("dve", st[:, 8:9], st[:, 4:5], -1.0, st[:, 7:8], ALU.mult, ALU.mult, [ST], [ST])
                        act(vn[:], za[:, 512:1024], AF.Identity, [ZA, ST], [VN], scale=st[:, 7:8], bias=st[:, 8:9])
                        tt("dve", vn[:], vn[:], lng[:], ALU.mult, [VN, CONST], [VN])
                        tt("dve", va[s][:], vn[:], lnb[:], ALU.add, [VN, CONST], [VA[s]])
                        if kind == "samp":
                            dma("pool", vas_o, va[s][:], [VA[s]], [], "va%d" % s)
                        cp("pool", vab[:], va[s][:], [VA[s]], [VAB])
                        if ASTOP < 5:
                            continue
                        w_ = wsTs if kind == "samp" else wsT
                        b_ = bsps if kind == "samp" else bsp
                        for h in range(4):
                            mm(psg[:, h * 128:(h + 1) * 128], w_[:, h, :], vab[:, h * 128:(h + 1) * 128], h == 0, h == 3, [CONST, VAB], [PSG])
                        for h in range(4):
                            stt("dve", outa[:, h * 128:(h + 1) * 128], psg[:, h * 128:(h + 1) * 128], b_[:, h:h + 1], za[:, h * 128:(h + 1) * 128],
                                ALU.add, ALU.mult, [PSG, CONST, ZA], [OA])
                        if ASTOP < 6:
                            continue
                        act(junk[:, 0:512], outa[:], AF.Square, [OA], [ST], accum_out=st[:, 9:10])
                        rstd(st[:, 10:11], st[:, 9:10], 512, ST)
                        stt("dve", nab[:], outa[:], st[:, 10:11], goa[:], ALU.mult, ALU.mult, [OA, ST, CONST], [NAB])
                        for c in range(4):
                            tr(ptr[:, c, :], nab[:, c * 128:(c + 1) * 128], ident[:], [NAB, CONST], [PTR])
                        cp("dve", nast[s][:], ptr[:, 0:4, :], [PTR], [NAST[s]])
                        dma("pool", nas.rearrange("c p t -> p c t")[:, :, oi * 128:(oi + 1) * 128], nast[s][:], [NAST[s]], [], "nast%d" % s)
                    P.flush()

                if LEVEL < 2:
                    return nc
                with ExitStack() as ph:
                    ebuf = [sb(ph, "e%d" % i, [128, 2, 512], F32) for i in range(2)]
                    EB = [Buf() for _ in range(2)]
                    spb = [sb(ph, "sp%d" % i, [128, 2, 512], BF16) for i in range(2)]
                    SPB = [Buf() for _ in range(2)]
                    wtb = [sb(ph, "wt%d" % i, [128, 2, 512], BF16) for i in range(2)]
                    WTB = [Buf() for _ in range(2)]
                    srun = sb(ph, "srun", [128, 2, 512], BF16)
                    SR = Buf()
                    sqf = sb(ph, "sqf", [128, 512], BF16)
                    SQF = Buf()
                    pz = [ps(ph, "pz%d" % i, [128, 2, 512]) for i in range(2)]
                    PZ = [Buf() for _ in range(2)]
                    pa = ps(ph, "pa", [128, 2, 512])
                    PA = Buf()
                    po = ps(ph, "po", [128, 512])
                    PO = Buf()
                    pss = ps(ph, "pss", [128, NOWN, 2])
                    add("dve", lambda e: e.memset(pss[:], 0.0), [], [SSB])

                    def attention(steps, finish):
                        n = len(steps)

                        def zmm(dst, DST, stp):
                            banks = set()
                            for sg in stp["segs"]():
                                mm(sg["zout"](dst), sg["kT"], sg["q"], sg["bank"] not in banks, False, stp["reads"], [DST])
                                banks.add(sg["bank"])
                                if sg.get("mask") is not None:
                                    mm(sg["mout"](dst), sg["mlhs"], sg["mask"], False, False, [CONST], [DST])

                        def view(t, stp):
                            return stp["view"](t)

                        def pvmm(si, last):
                            pvs = steps[si]
                            if SKIPPV:
                                return
                            for (o_, l_, r_, first, tp) in pvs["pv"](po, wtb[si % 2]):
                                mm(o_, l_, r_, si == 0 and first, last, pvs["reads"] + [WTB[si % 2]], [PO], tile_position=tp)

                        add("dve", lambda e: e.memset(srun[:], 0.0), [], [SR])
                        zmm(pz[0], PZ[0], steps[0])
                        act(view(ebuf[0], steps[0]), view(pz[0], steps[0]), AF.Exp, [PZ[0]], [EB[0]])
                        act(view(spb[0], steps[0]), view(ebuf[0], steps[0]), AF.Ln, [EB[0]], [SPB[0]], bias=1.0)
                        for s_ in range(n):
                            b = s_ % 2
                            nb = (s_ + 1) % 2
                            stp = steps[s_]
                            if s_ + 1 < n:
                                nx = steps[s_ + 1]
                                zmm(pz[nb], PZ[nb], nx)
                                act(view(ebuf[nb], nx), view(pz[nb], nx), AF.Exp, [PZ[nb]], [EB[nb]])
                                act(view(spb[nb], nx), view(ebuf[nb], nx), AF.Ln, [EB[nb]], [SPB[nb]], bias=1.0)
                            zmm(pa, PA, stp)
                            for (o_, l_, r_) in stp["cum"](pa, spb[b]):
                                mm(o_, l_, r_, False, False, [CONST, SPB[b]], [PA])
                            if s_ > 0:
                                for (o_, l_, r_) in stp["carry"](pa, srun):
                                    mm(o_, l_, r_, False, False, [CONST, SR], [PA])
                            act(view(wtb[b], stp), view(pa, stp), AF.Exp, [PA], [WTB[b]])
                            tt("dve", view(srun, stp), view(srun, stp), view(spb[b], stp), ALU.add, [SR, SPB[b]], [SR])
                            if s_ > 0:
                                pvmm(s_ - 1, False)
                        pvmm(n - 1, True)
                        if SKIPPV < 2:
                            finish()

                    with ExitStack() as ph0:
                        ckb = sb(ph0, "ckb", [128, 8, 512], BF16)
                        CKB = Buf()
                        kTs = sb(ph0, "kTs", [128, 4, 2048], BF16)
                        KTSB = Buf()
                        vsb = sb(ph0, "vsb", [128, 16, 512], BF16)
                        VSB = Buf()
                        sqs = sb(ph0, "sqs", [128, 4, 128], BF16)
                        SQS = Buf()
                        ptk = po[:].bitcast(BF16).rearrange("p (j t) -> p j t", t=128)
                        for bi in range(4):
                            for half in range(2):
                                dma("pool", vsb[:, half * 8:(half + 1) * 8, :], cv[bi, half * 1024:(half + 1) * 1024, :].rearrange("(n p) f -> p n f", p=128), [], [VSB], "vsb")
                            for half in range(2):
                                dma("pool", ckb[:], ck[bi, half * 1024:(half + 1) * 1024, :].rearrange("(n p) f -> p n f", p=128), [], [CKB], "ckb")
                                for hp in range(4):
                                    for j in range(8):
                                        tr(ptk[:, j, :], ckb[:, j, hp * 128:(hp + 1) * 128], ident[:], [CKB, CONST], [PO])
                                    act(kTs[:, hp, half * 1024:(half + 1) * 1024], ptk.rearrange("p j t -> p (j t)"), AF.Copy, [PO], [KTSB], scale=0.125)
                            qc = SAMP * 128 + 32 * bi
                            steps = []
                            for kp in [16] + list(range(15, -1, -1)):
                                new = kp == 16
                                KPp = 128

                                def segs(new=new, kp=kp, KPp=KPp, qc=qc, bi=bi):
                                    out = []
                                    for h in range(8):
                                        hp, par = h // 2, h % 2
                                        r0 = 64 * par
                                        if new:
                                            kT_ = kTn[r0:r0 + 64, hp, :]
                                        else:
                                            kT_ = kTs[r0:r0 + 64, hp, kp * 128:(kp + 1) * 128]
                                        sg = {"kT": kT_, "q": qT[r0:r0 + 64, hp, qc:qc + 32], "bank": par,
                                              "zout": (lambda d, hp=hp, par=par: d[:, par, hp * 32:(hp + 1) * 32])}
                                        if new and h >= 6:
                                            sg["mask"] = masks[:, bi, 0:4].rearrange("p h q -> p (h q)")
                                            sg["mlhs"] = ident[:]
                                            sg["mout"] = (lambda d, par=par: d[:, par, 0:128])
                                        out.append(sg)
                                    return out

                                def pvf(po_, wt_, new=new, kp=kp, KPp=KPp, bi=bi):
                                    out = []
                                    for h in range(8):
                                        hp, par = h // 2, h % 2
                                        l_ = vnew[:, h * 64:(h + 1) * 64] if new else vsb[:, kp, h * 64:(h + 1) * 64]
                                        out.append((po_[64 * par:64 * par + 64, hp * 32:(hp + 1) * 32], l_, wt_[:, par, hp * 32:(hp + 1) * 32],
                                                    h < 2, (0, 64 * par)))
                                    return out

                                steps.append({
                                    "reads": [KTN, VNEW, QT[SAMP]] if new else [KTSB, VSB, QT[SAMP]],
                                    "view": (lambda t: t[:, :, 0:128]),
                                    "cum": (lambda pa_, sp_: [(pa_[:, par, 0:128], negT[:], sp_[:, par, 0:128]) for par in range(2)]),
                                    "carry": (lambda pa_, sr_: [(pa_[:, par, 0:128], negO[:], sr_[:, par, 0:128]) for par in range(2)]),
                                    "pv": pvf,
                                    "segs": segs,
                                })

                            def fin(bi=bi, qc=qc):
                                act(sqs[:, :, 32 * bi:32 * bi + 32], po[:, 0:128].rearrange("p (c q) -> p c q", q=32), AF.Square, [PO], [SQS])
                                for hp in range(4):
                                    act(obT[:, hp, qc:qc + 32], po[:, hp * 32:(hp + 1) * 32], AF.Identity, [PO, CONST], [OBT], scale=gob[:, hp:hp + 1])

                            if B0MODE >= 1:
                                attention(steps[:B0MODE], fin)
                        for hp in range(4):
                            mm(pss[:, SAMP, :], sqs[:, hp, :], onesf[:, 0:2], False, False, [SQS, CONST], [SSB])
                        P.flush()

                    if LEVEL < 3:
                        return nc
                    kTp = sb(ph, "kTp", [128, NPOS * 128], BF16)
                    KTP = Buf()
                    vp = sb(ph, "vp", [128, NPOS, 128], BF16)
                    VP = Buf()
                    groups = [[0]] + [[1 + 4 * g + j for j in range(4)] for g in range(NPR // 4)]
                    for hp in range(4):
                        dma("sp", kTp[:], kts[hp], [], [KTP], "kTp")
                        dma("sp", vp[:], vss[hp], [], [VP], "vp")
                        for grp in groups:
                            n = len(grp)
                            W = n * 128
                            p0 = HALO + grp[0]
                            qc = grp[0] * 128
                            steps = []
                            for kp in range(p0 + n - 1, -1, -1):
                                c0 = (kp - p0) * 128 if kp >= p0 else 0
                                diag = kp >= p0

                                def segs(kp=kp, c0=c0, diag=diag, W=W, qc=qc, hp=hp):
                                    out = []
                                    for par in range(2):
                                        r0 = 64 * par
                                        sg = {"kT": kTp[r0:r0 + 64, kp * 128:(kp + 1) * 128], "q": qT[r0:r0 + 64, hp, qc + c0:qc + W], "bank": par,
                                              "zout": (lambda d, par=par, c0=c0, W=W: d[:, par, c0:W])}
                                        if diag:
                                            sg["mask"] = maskb[:]
                                            sg["mlhs"] = ident[:]
                                            sg["mout"] = (lambda d, par=par, c0=c0: d[:, par, c0:c0 + 128])
                                        out.append(sg)
                                    return out

                                def pvf(po_, wt_, kp=kp, c0=c0, W=W):
                                    return [(po_[64 * par:64 * par + 64, c0:W], vp[:, kp, 64 * par:64 * par + 64], wt_[:, par, c0:W],
                                             True, (0, 64 * par)) for par in range(2)]

                                steps.append({
                                    "reads": [KTP, VP] + [QT[o] for o in grp],
                                    "view": (lambda t, c0=c0, W=W: t[:, :, c0:W]),
                                    "cum": (lambda pa_, sp_, c0=c0, W=W: [(pa_[:, par, c0:W], negT[:], sp_[:, par, c0:W]) for par in range(2)]),
                                    "carry": (lambda pa_, sr_, c0=c0, W=W: [(pa_[:, par, c0:W], negO[:], sr_[:, par, c0:W]) for par in range(2)]),
                                    "pv": pvf,
                                    "segs": segs,
                                })

                            def fin(grp=grp, W=W, qc=qc, hp=hp):
                                act(sqf[:, 0:W], po[:, 0:W], AF.Square, [PO], [SQF])
                                act(obT[:, hp, qc:qc + W], po[:, 0:W], AF.Identity, [PO, CONST], [OBT], scale=gob[:, hp:hp + 1])
                                for j, o in enumerate(grp):
                                    mm(pss[:, o, :], sqf[:, j * 128:(j + 1) * 128], onesf[:, 0:2], False, False, [SQF, CONST], [SSB])

                            attention(steps, fin)
                    cp("dve", ssb[:], pss[:, :, 0], [SSB], [SSB])
                    P.flush()
            if LEVEL < 4:
                return nc
            X1S = Buf("x1s")
            H2S = Buf("h2s")
            with ExitStack() as ph:
                wo = sb(ph, "wo", [128, 8, 1024], BF16)
                WO = Buf()
                for c in range(8):
                    dma("pool", wo[:, c, :], w_out[c * 128:(c + 1) * 128, :], [], [WO], "wo")
                nat = [sb(ph, "nat%d" % i, [128, 4, 128], BF16) for i in range(2)]
                NATB = [Buf() for _ in range(2)]
                xt = [sb(ph, "cxt%d" % i, [128, 1024], F32) for i in range(2)]
                XT = [Buf() for _ in range(2)]
                mixa = sb(ph, "mixa", [128, 1024], F32)
                MA = Buf()
                mix = sb(ph, "mix", [128, 1024], F32)
                MX = Buf()
                x1 = [sb(ph, "x1_%d" % i, [128, 1024], F32) for i in range(2)]
                X1 = [Buf() for _ in range(2)]
                junk = sb(ph, "cjunk", [128, 1024], BF16)
                JK = Buf()
                st = sb(ph, "cstat", [128, 8], F32)
                ST = Buf()
                hn2 = sb(ph, "hn2", [128, 1024], BF16)
                HN2 = Buf()
                h2st = [sb(ph, "h2st%d" % i, [128, 8, 128], BF16) for i in range(2)]
                H2ST = [Buf() for _ in range(2)]
                pma = ps(ph, "pma", [128, 1024])
                PMA = Buf()
                pmb = ps(ph, "pmb", [128, 1024])
                PMB = Buf()
                ptr = ps(ph, "cptr", [128, 8, 128], BF16)
                PTR = Buf()
                for oi in range(NOWN):
                    s = oi % 2
                    src = xs if oi == SAMP else xk[(HALO + oi) * 128:(HALO + oi + 1) * 128, :]
                    dma("sp", xt[s][:], src, [], [XT[s]], "cxt%d" % s)
                    cols = slice(oi * 128, (oi + 1) * 128)
                    dma("sp", nat[s][:], nas.rearrange("c p t -> p c t")[:, :, cols], [], [NATB[s]], "nat%d" % s)
                    for hf in range(2):
                        for c in range(4):
                            mm(pma[:, hf * 512:(hf + 1) * 512], nat[s][:, c, :], wo[:, c, hf * 512:(hf + 1) * 512], c == 0, c == 3, [NATB[s], WO], [PMA])
                    for hf in range(2):
                        for c in range(4):
                            mm(pmb[:, hf * 512:(hf + 1) * 512], obT[:, c, cols], wo[:, 4 + c, hf * 512:(hf + 1) * 512], c == 0, c == 3, [OBT, WO], [PMB])
                    rstd(st[:, 0:1], ssb[:, oi:oi + 1], 512, ST if oi else ST)
                    act(mixa[:], pma[:], AF.Copy, [PMA], [MA])
                    stt("dve", mix[:], pmb[:], st[:, 0:1], mixa[:], ALU.mult, ALU.add, [PMB, ST, MA, SSB], [MX])
                    act(junk[:], mix[:], AF.Square, [MX], [ST], accum_out=st[:, 1:2])
                    rstd(st[:, 2:3], st[:, 1:2], 1024, ST)
                    stt("dve", mix[:], mix[:], st[:, 2:3], gpost[:], ALU.mult, ALU.mult, [MX, ST, CONST], [MX])
                    tt("dve", x1[s][:], mix[:], xt[s][:], ALU.add, [MX, XT[s]], [X1[s]])
                    dma("pool", x1s[oi], x1[s][:], [X1[s]], [], "x1st%d" % s)
                    act(junk[:], x1[s][:], AF.Square, [X1[s]], [ST], accum_out=st[:, 3:4])
                    rstd(st[:, 4:5], st[:, 3:4], 1024, ST)
                    stt("dve", hn2[:], x1[s][:], st[:, 4:5], gpf[:], ALU.mult, ALU.mult, [X1[s], ST, CONST], [HN2])
                    for c in range(8):
                        tr(ptr[:, c, :], hn2[:, c * 128:(c + 1) * 128], ident[:], [HN2, CONST], [PTR])
                    cp("dve", h2st[s][:], ptr[:], [PTR], [H2ST[s]])
                    dma("pool", h2s.rearrange("c p t -> p c t")[:, :, oi * 128:(oi + 1) * 128], h2st[s][:], [H2ST[s]], [], "h2st%d" % s)
                P.flush()
        if LEVEL < 5:
            return nc
        with ExitStack() as ph:
            wd = sb(ph, "wd", [128, 22, 1024], BF16)
            WD = Buf()
            for c in range(22):
                dma("pool", wd[:, c, :], w_down[c * 128:(c + 1) * 128, :], [], [WD], "wd")
            hg = [sb(ph, "hg%d" % i, [128, 8, 512], BF16) for i in range(2)]
            HG = [Buf() for _ in range(2)]
            wuf = [sb(ph, "wuf%d" % i, [128, 2, 8, 128], BF16) for i in range(3)]
            WUF = [Buf() for _ in range(3)]
            ug = [sb(ph, "ug%d" % i, [128, 514], F32) for i in range(2)]
            UG = [Buf() for _ in range(2)]
            uv = [sb(ph, "uv%d" % i, [128, 514], F32) for i in range(2)]
            UV = [Buf() for _ in range(2)]
            yg = [sb(ph, "yg%d" % i, [128, 512], F32) for i in range(2)]
            YG = [Buf() for _ in range(2)]
            yv = [sb(ph, "yv%d" % i, [128, 512], F32) for i in range(2)]
            YV = [Buf() for _ in range(2)]
            gl = [sb(ph, "gl%d" % i, [128, 512], F32) for i in range(2)]
            GL = [Buf() for _ in range(2)]
            aT = sb(ph, "aT", [128, 22, 512], BF16)
            AT = Buf()
            HIST = Buf()
            x1t = [sb(ph, "x1t%d" % i, [128, 1024], F32) for i in range(2)]
            X1T = [Buf() for _ in range(2)]
            yo = [sb(ph, "yo%d" % i, [128, 1024], F32) for i in range(2)]
            YO = [Buf() for _ in range(2)]
            tmst = [sb(ph, "tmst%d" % i, [128, 256], F32) for i in range(2)]
            TMST = [Buf() for _ in range(2)]
            junk = sb(ph, "djunk", [128, 1024], BF16)
            JK = Buf()
            st = sb(ph, "dstat", [128, 4], F32)
            ST = Buf()
            pg = [ps(ph, "pg%d" % i, [128, 512]) for i in range(2)]
            PG = [Buf() for _ in range(2)]
            pvv = [ps(ph, "pvv%d" % i, [128, 512]) for i in range(2)]
            PVV = [Buf() for _ in range(2)]
            pf = ps(ph, "pf", [128, 1024])
            PF = Buf()
            pt = ps(ph, "pt", [128, 256])
            PT = Buf()
            h2v = h2s.rearrange("c p t -> p c t")
            cnt = {"g": 0, "f": 0, "t": 0, "x": 0}

            def load_wuf(fc):
                s = cnt["f"] % 3
                cnt["f"] += 1
                wv = wups.rearrange("c p n -> p c n")
                dma("sp", wuf[s][:, 0], wv[:, :, fc * 128:(fc + 1) * 128], [], [WUF[s]], "wuf%d" % s)
                dma("sp", wuf[s][:, 1], wv[:, :, (22 + fc) * 128:(23 + fc) * 128], [], [WUF[s]], "wuf%d" % s)
                return s

            gs = cnt["g"] % 2
            cnt["g"] += 1
            dma("sp", hg[gs][:, :, 0:2], h2v[:, :, 126:128], [], [HG[gs]], "hg%d" % gs)
            for fc in range(22):
                ws = load_wuf(fc)
                for typ in range(2):
                    ch = fc + 22 * typ
                    for c in range(8):
                        mm(pg[0][:, ch * 2:ch * 2 + 2], wuf[ws][:, typ, c, :], hg[gs][:, c, 0:2], fc == 0 and typ == 0 and c == 0, fc == 21 and typ == 1 and c == 7,
                           [WUF[ws], HG[gs]], [PG[0]])
            cp("dve", hist[:], pg[0][:, 0:88].rearrange("p (f t) -> p f t", t=2), [PG[0]], [HIST])

            def ffn_group(ois, sample, tm_tile, out_fn):
                n = len(ois)
                W = n * 128
                gs = cnt["g"] % 2
                cnt["g"] += 1
                dma("sp", hg[gs][:, :, 0:W], h2v[:, :, ois[0] * 128:ois[0] * 128 + W], [], [HG[gs]], "hg%d" % gs)
                for fc in range(22):
                    ws = load_wuf(fc)
                    b = fc % 2
                    for typ, (pp, PP, uu, UU, yy, YY) in enumerate([(pg[b], PG[b], ug[b], UG[b], yg[b], YG[b]), (pvv[b], PVV[b], uv[b], UV[b], yv[b], YV[b])]):
                        ch = fc + 22 * typ
                        for c in range(8):
                            mm(pp[:, 0:W], wuf[ws][:, typ, c, :], hg[gs][:, c, 0:W], c == 0, c == 7, [WUF[ws], HG[gs]], [PP])
                        act(uu[:, 2:2 + W], pp[:, 0:W], AF.Copy, [PP], [UU])
                        act(yy[:, 0:W], pp[:, 0:W], AF.Identity, [PP, CONST], [YY], scale=cvec[:, ch, 2:3], bias=cvec[:, ch, 3:4])
                        if sample:
                            cp("pool", uu[:, 0:128].rearrange("p (i c) -> p i c", c=32)[:, :, 0:2], cvec[:, ch, 4:12].rearrange("p (i c) -> p i c", c=2), [CONST, UU], [UU])
                        else:
                            cp("pool", uu[:, 0:2], hist[:, ch, :], [HIST, UU], [UU])
                            cp("pool", hist[:, ch, :], uu[:, W:W + 2], [UU, HIST], [HIST])
                        stt("dve", yy[:, 0:W], uu[:, 1:1 + W], cvec[:, ch, 1:2], yy[:, 0:W], ALU.mult, ALU.add, [UU, CONST, YY], [YY])
                        stt("dve", yy[:, 0:W], uu[:, 0:W], cvec[:, ch, 0:1], yy[:, 0:W], ALU.mult, ALU.add, [UU, CONST, YY], [YY])
                    act(gl[b][:, 0:W], yg[b][:, 0:W], AF.Gelu_apprx_tanh, [YG[b]], [GL[b]])
                    tt("pool", aT[:, fc, 0:W], gl[b][:, 0:W], yv[b][:, 0:W], ALU.mult, [GL[b], YV[b]], [AT])
                    if tm_tile is not None:
                        j, dst = tm_tile
                        ts_ = cnt["t"] % 2
                        cnt["t"] += 1
                        for typ in range(2):
                            for c in range(8):
                                mm(pt[:, typ * 128:(typ + 1) * 128], hg[gs][:, c, j * 128:(j + 1) * 128], wuf[ws][:, typ, c, :],
                                   typ == 0 and c == 0, typ == 1 and c == 7, [WUF[ws], HG[gs]], [PT])
                        act(tmst[ts_][:], pt[:], AF.Copy, [PT], [TMST[ts_]])
                        dma("pool", dst[:, fc * 128:(fc + 1) * 128], tmst[ts_][:, 0:128], [TMST[ts_]], [], "tmst%d" % ts_)
                        dma("pool", dst[:, (22 + fc) * 128:(23 + fc) * 128], tmst[ts_][:, 128:256], [TMST[ts_]], [], "tmst%d" % ts_)
                for j, oi in enumerate(ois):
                    xs_ = cnt["x"] % 2
                    cnt["x"] += 1
                    dma("sp", x1t[xs_][:], x1s[oi], [], [X1T[xs_]], "x1t%d" % xs_)
                    for hf in range(2):
                        for fc in range(22):
                            mm(pf[:, hf * 512:(hf + 1) * 512], aT[:, fc, j * 128:(j + 1) * 128], wd[:, fc, hf * 512:(hf + 1) * 512], fc == 0, fc == 21, [AT, WD], [PF])
                    act(junk[:], pf[:], AF.Square, [PF], [ST], accum_out=st[:, 0:1])
                    rstd(st[:, 1:2], st[:, 0:1], 1024, ST)
                    stt("dve", yo[xs_][:], pf[:], st[:, 1:2], gpostf[:], ALU.mult, ALU.mult, [PF, ST, CONST], [YO[xs_]])
                    tt("dve", yo[xs_][:], yo[xs_][:], x1t[xs_][:], ALU.add, [YO[xs_], X1T[xs_]], [YO[xs_]])
                    dma("pool", out_fn(oi), yo[xs_][:], [YO[xs_]], [], "yo%d" % xs_)

            for g in range(NPR // 4):
                ois = [1 + 4 * g + j for j in range(4)]
                ffn_group(ois, False, (3, upl_o) if g == NPR // 4 - 1 else None, lambda oi: y_o[(oi - 1) * 128:oi * 128, :])
            ffn_group([SAMP], True, (0, ups_o), lambda oi: ys_o)
            P.flush()
      except _Stop:
        pass
    return nc


_NC = None


def kernel(**inputs):
    global _NC
    f = lambda k: np.ascontiguousarray(np.asarray(inputs[k], dtype=np.float32))
    xp, xsm = f("x_prompt"), f("x_sample")
    ckf = f("cache_sb_k")[0].reshape(32, 2048, 512)
    cvf = f("cache_sb_v")[0].reshape(32, 2048, 512)
    ccf = f("cache_ffn_conv")[0]
    shared = {
        "w_in": f("w_in")[0], "w_out": f("w_out")[0], "w_up": f("w_up")[0], "w_down": f("w_down")[0],
        "g_pre_mix": f("g_pre_mix")[0], "ln_v_g": f("ln_v_g")[0], "ln_v_b": f("ln_v_b")[0],
        "w_spatial": f("w_spatial")[0], "b_spatial": f("b_spatial")[0], "g_out_a": f("g_out_a")[0],
        "g_out_b": f("g_out_b")[0], "g_post_mix": f("g_post_mix")[0], "g_pre_ffn": f("g_pre_ffn")[0],
        "conv_w": f("conv_w")[0], "conv_b": f("conv_b"), "g_post_ffn": f("g_post_ffn")[0],
    }
    in_maps = []
    for c in range(8):
        b, half = c // 2, c % 2
        xk = np.zeros((NPOS * 128, 1024), np.float32)
        if half == 0:
            xk[4096:] = xp[b, 0:4096]
        else:
            xk[:] = xp[b]
        xs = np.zeros((128, 1024), np.float32)
        for i in range(4):
            xs[32 * i:32 * i + 16] = xsm[4 * c + i]
        m = dict(shared)
        m.update({"xk": xk, "xs": xs, "ck": np.ascontiguousarray(ckf[4 * c:4 * c + 4]), "cv": np.ascontiguousarray(cvf[4 * c:4 * c + 4]),
                  "cconv": np.ascontiguousarray(ccf[4 * c:4 * c + 4].reshape(8, 5632))})
        in_maps.append(m)
    if _NC is None:
        _NC = build()
    res = run_bass_kernel_spmd(_NC, in_maps, core_ids=list(range(8)))
    R = res.results
    yp = np.zeros((4, 8192, 1024), np.float32)
    ys = np.zeros((32, 16, 1024), np.float32)
    kp = np.zeros((1, 4, 8192, 8, 64), np.float32)
    vp = np.zeros((1, 4, 8192, 8, 64), np.float32)
    ksn = np.zeros((1, 32, 16, 8, 64), np.float32)
    vsn = np.zeros((1, 32, 16, 8, 64), np.float32)
    vas = np.zeros((1, 32, 16, 512), np.float32)
    cpp = np.zeros((1, 4, 2, 5632), np.float32)
    css = np.zeros((1, 32, 2, 5632), np.float32)
    for c in range(8):
        b, half = c // 2, c % 2
        r = R[c]
        sl = slice(half * 4096, (half + 1) * 4096)
        yp[b, sl] = r["y"]
        kp[0, b, sl] = r["ko"].reshape(4096, 8, 64)
        vp[0, b, sl] = r["vo"].reshape(4096, 8, 64)
        if half == 1:
            cpp[0, b] = r["upl"][126:128]
        for i in range(4):
            rows = slice(32 * i, 32 * i + 16)
            ys[4 * c + i] = r["ys"][rows]
            ksn[0, 4 * c + i] = r["kso"][rows].reshape(16, 8, 64)
            vsn[0, 4 * c + i] = r["vso"][rows].reshape(16, 8, 64)
            vas[0, 4 * c + i] = r["vas"][rows]
            css[0, 4 * c + i] = r["ups"][32 * i + 14:32 * i + 16]
    return (yp, ys, kp, vp, ksn, vsn, vas, cpp, css)
```

```python
import numpy as np
from contextlib import ExitStack
import concourse.bass as bass
import concourse.mybir as mybir
from concourse.bass_utils import run_bass_kernel_spmd

F32 = mybir.dt.float32
BF16 = mybir.dt.bfloat16
AF = mybir.ActivationFunctionType
ALU = mybir.AluOpType
ENGS = ("sp", "act", "pool", "dve", "pe")
SEM_EPOCH = 8000

NPOS = 64
HALO = 31
NOWN = 34
SAMP = 33
EPS = 1e-6
NEG = -30000.0
LEVEL = 9
AKINDS = ("key", "own", "samp")
ASTOP = 99
B0MODE = 9
SKIPPV = 0


class _Stop(Exception):
    pass


class Buf:
    __slots__ = ("name", "w", "r")

    def __init__(self, name=""):
        self.name = name
        self.w = None
        self.r = []


class Op:
    __slots__ = ("id", "eng", "fn", "deps", "dmakey", "sem", "val", "awaited")


class Prog:
    def __init__(self, nc, es):
        self.nc = nc
        self.es = es
        self.ops = []
        self.start = 0
        self.dmasem = {}
        self.nsem = 0

    def add(self, eng, fn, reads=(), writes=(), dmakey=None):
        op = Op()
        op.id = len(self.ops)
        op.eng = eng
        op.fn = fn
        deps = set()
        for b in reads:
            if b.w is not None:
                deps.add(b.w)
        for b in writes:
            if b.w is not None:
                deps.add(b.w)
            deps.update(b.r)
        op.deps = {d for d in deps if d >= self.start}
        op.dmakey = dmakey
        op.sem = None
        op.val = 0
        op.awaited = False
        for b in reads:
            b.r.append(op.id)
        for b in writes:
            b.w = op.id
            b.r = []
        self.ops.append(op)
        return op

    def flush(self):
        nc = self.nc
        ops = self.ops
        cur = ops[self.start:]
        if not cur:
            return
        for op in cur:
            for d in op.deps:
                dop = ops[d]
                if dop.eng == "pe" and op.eng == "pe" and dop.dmakey is None:
                    continue
                dop.awaited = True
        per = {e: [o for o in cur if o.eng == e] for e in ENGS}
        last = {}
        for e in ENGS:
            comp = [o for o in per[e] if o.dmakey is None and o.fn is not None]
            if comp:
                comp[-1].awaited = True
                last[e] = comp[-1]
        for e in ENGS:
            n = 0
            sem = None
            for op in per[e]:
                if op.fn is None:
                    continue
                if op.dmakey is not None:
                    if op.dmakey not in self.dmasem:
                        self.nsem += 1
                        self.dmasem[op.dmakey] = [self.es.enter_context(nc.semaphore("sd%d" % self.nsem)), 0]
                    ent = self.dmasem[op.dmakey]
                    ent[1] += 16
                    op.sem, op.val = ent[0], ent[1]
                elif op.awaited:
                    if sem is None or n >= SEM_EPOCH:
                        self.nsem += 1
                        sem = self.es.enter_context(nc.semaphore("se%d" % self.nsem))
                        n = 0
                    n += 1
                    op.sem, op.val = sem, n
        barrier = [(o.sem, o.val) for o in last.values()]
        barrier += [(s, v) for (s, v) in self.dmasem.values()]

        def run(ename, e):
            seen = {}

            def wait(sem, val):
                k = id(sem)
                if seen.get(k, 0) >= val:
                    return
                seen[k] = val
                e.wait_ge(sem, val)

            for op in per[ename]:
                need = {}
                for d in sorted(op.deps):
                    dop = ops[d]
                    if dop.sem is None:
                        continue
                    if dop.eng == "pe" and ename == "pe" and dop.dmakey is None:
                        continue
                    k = id(dop.sem)
                    if k not in need or need[k][1] < dop.val:
                        need[k] = (dop.sem, dop.val)
                for (s_, v_) in need.values():
                    wait(s_, v_)
                if op.fn is None:
                    continue
                ins = op.fn(e)
                if op.dmakey is not None:
                    ins.then_inc(op.sem, 16)
                elif op.awaited:
                    ins.then_inc(op.sem, 1)
            for (s, v) in barrier:
                wait(s, v)

        with nc.Block() as block:
            @block.sync
            def _(e):
                run("sp", e)

            @block.scalar
            def _(e):
                run("act", e)

            @block.gpsimd
            def _(e):
                run("pool", e)

            @block.vector
            def _(e):
                run("dve", e)

            @block.tensor
            def _(e):
                run("pe", e)
        self.start = len(ops)


def build():
    nc = bass.Bass("TRN2", target_bir_lowering=False)
    di = lambda n, s: nc.dram_tensor(n, list(s), F32, kind="ExternalInput").ap()
    do = lambda n, s: nc.dram_tensor(n, list(s), F32, kind="ExternalOutput").ap()
    dscr = lambda n, s, dt: nc.dram_tensor(n, list(s), dt, kind="Internal").ap()

    xk = di("xk", [NPOS * 128, 1024])
    xs = di("xs", [128, 1024])
    ck = di("ck", [4, 2048, 512])
    cv = di("cv", [4, 2048, 512])
    cconv = di("cconv", [8, 5632])
    w_in = di("w_in", [1024, 2560])
    w_out = di("w_out", [1024, 1024])
    w_up = di("w_up", [1024, 5632])
    w_down = di("w_down", [2816, 1024])
    g_pre_mix = di("g_pre_mix", [1024])
    ln_v_g = di("ln_v_g", [512])
    ln_v_b = di("ln_v_b", [512])
    w_spatial = di("w_spatial", [4, 128, 128])
    b_spatial = di("b_spatial", [4, 128])
    g_out_a = di("g_out_a", [512])
    g_out_b = di("g_out_b", [512])
    g_post_mix = di("g_post_mix", [1024])
    g_pre_ffn = di("g_pre_ffn", [1024])
    conv_w = di("conv_w", [3, 5632])
    conv_b = di("conv_b", [1, 5632])
    g_post_ffn = di("g_post_ffn", [1024])

    NPR = NOWN - 2
    y_o = do("y", [NPR * 128, 1024])
    ys_o = do("ys", [128, 1024])
    ko_o = do("ko", [NPR * 128, 512])
    vo_o = do("vo", [NPR * 128, 512])
    kso_o = do("kso", [128, 512])
    vso_o = do("vso", [128, 512])
    vas_o = do("vas", [128, 512])
    upl_o = do("upl", [128, 5632])
    ups_o = do("ups", [128, 5632])

    kts = dscr("kts", [4, 128, NPOS * 128], BF16)
    vss = dscr("vss", [4, 128, NPOS, 128], BF16)
    x1s = dscr("x1s", [NOWN, 128, 1024], F32)
    h2s = dscr("h2s", [8, 128, NOWN * 128], BF16)
    wups = dscr("wups", [8, 128, 5632], BF16)
    nas = dscr("nas", [4, 128, NOWN * 128], BF16)

    with ExitStack() as es0:
      es = es0.enter_context(ExitStack())
      try:
        P = Prog(nc, es)
        add = P.add
        OUT = Buf("outs")

        def sb(st, n, s, d):
            return st.enter_context(nc.sbuf_tensor(n, list(s), d))

        def ps(st, n, s, d=F32):
            return st.enter_context(nc.psum_tensor(n, list(s), d))

        def dma(q, out, in_, reads, writes, key):
            add(q, lambda e: e.dma_start(out=out, in_=in_), reads, writes, dmakey=key)

        def act(out, in_, func, reads, writes, **kw):
            add("act", lambda e: e.activation(out=out, in_=in_, func=func, **kw), reads, writes)

        def mm(out, lhsT, rhs, start, stop, reads, writes, tile_position=None):
            if tile_position is None:
                add("pe", lambda e: e.matmul(out, lhsT=lhsT, rhs=rhs, start=start, stop=stop, skip_group_check=True), reads, writes)
            else:
                add("pe", lambda e: e.matmul(out, lhsT=lhsT, rhs=rhs, start=start, stop=stop, skip_group_check=True, tile_position=tile_position), reads, writes)

        def tr(out, in_, ident, reads, writes):
            add("pe", lambda e: e.transpose(out=out, in_=in_, identity=ident), reads, writes)

        def stt(eng, out, in0, scalar, in1, op0, op1, reads, writes):
            add(eng, lambda e: e.scalar_tensor_tensor(out=out, in0=in0, scalar=scalar, in1=in1, op0=op0, op1=op1), reads, writes)

        def ts(eng, out, in0, s1, s2, op0, op1, reads, writes):
            if s2 is None:
                add(eng, lambda e: e.tensor_scalar(out=out, in0=in0, scalar1=s1, scalar2=None, op0=op0), reads, writes)
            else:
                add(eng, lambda e: e.tensor_scalar(out=out, in0=in0, scalar1=s1, scalar2=s2, op0=op0, op1=op1), reads, writes)

        def tt(eng, out, in0, in1, op, reads, writes):
            add(eng, lambda e: e.tensor_tensor(out=out, in0=in0, in1=in1, op=op), reads, writes)

        def cp(eng, out, in_, reads, writes):
            add(eng, lambda e: e.tensor_copy(out=out, in_=in_), reads, writes)

        def rstd(r, ssum, n, b):
            ts("dve", r, ssum, 1.0 / n, EPS, ALU.mult, ALU.add, [b], [b])
            act(r, r, AF.Sqrt, [b], [b])
            add("dve", lambda e: e.reciprocal(out=r, in_=r), [b], [b])

        ident = sb(es, "ident", [128, 128], BF16)
        identf = sb(es, "identf", [128, 128], F32)
        negT = sb(es, "negT", [128, 128], BF16)
        negO = sb(es, "negO", [128, 128], BF16)
        maskb = sb(es, "maskb", [128, 128], BF16)
        masks = sb(es, "masks", [128, 4, 8, 32], BF16)
        onesf = sb(es, "onesf", [128, 2], BF16)
        tmpf = sb(es, "tmpf", [128, 128], F32)
        gpre = sb(es, "gpre", [128, 1024], F32)
        lng = sb(es, "lng", [128, 512], F32)
        lnb = sb(es, "lnb", [128, 512], F32)
        goa = sb(es, "goa", [128, 512], F32)
        gpost = sb(es, "gpost", [128, 1024], F32)
        gpf = sb(es, "gpf", [128, 1024], F32)
        gpostf = sb(es, "gpostf", [128, 1024], F32)
        gob = sb(es, "gob", [128, 4], F32)
        bsp = sb(es, "bsp", [128, 4], F32)
        bsps = sb(es, "bsps", [128, 4], F32)
        wsT = sb(es, "wsT", [128, 4, 128], BF16)
        wsTs = sb(es, "wsTs", [128, 4, 128], BF16)
        cvec = sb(es, "cvec", [128, 44, 12], F32)
        hist = sb(es, "hist", [128, 44, 2], F32)
        ssb = sb(es, "ssb", [128, NOWN], F32)
        CONST = Buf("const")

        with ExitStack() as ph:
            wn = sb(ph, "wn", [128, 4, 128], F32)
            wnb = sb(ph, "wnb", [128, 4, 128], BF16)
            cst = sb(ph, "cst", [12, 5632], F32)
            pst = ps(ph, "pst", [128, 4, 128], BF16)
            pc0 = ps(ph, "pc0", [128, 22, 12], F32)
            pc1 = ps(ph, "pc1", [128, 22, 12], F32)
            C2 = Buf("c2")
            add("pool", lambda e: e.memset(identf[:], 0.0), [], [CONST])
            add("pool", lambda e: e.affine_select(out=identf[:], in_=identf[:], pattern=[[-1, 128]], compare_op=ALU.not_equal, fill=1.0, base=0, channel_multiplier=1), [CONST], [CONST])
            cp("dve", ident[:], identf[:], [CONST], [CONST])
            add("pool", lambda e: e.memset(tmpf[:], -1.0), [], [C2])
            cp("dve", negO[:], tmpf[:], [C2], [CONST])
            add("pool", lambda e: e.affine_select(out=tmpf[:], in_=tmpf[:], pattern=[[-1, 128]], compare_op=ALU.is_ge, fill=0.0, base=0, channel_multiplier=1), [C2], [C2])
            cp("dve", negT[:], tmpf[:], [C2], [CONST])
            add("pool", lambda e: e.memset(tmpf[:], 0.0), [C2], [C2])
            add("pool", lambda e: e.affine_select(out=tmpf[:], in_=tmpf[:], pattern=[[1, 128]], compare_op=ALU.is_gt, fill=NEG, base=0, channel_multiplier=-1), [C2], [C2])
            cp("dve", maskb[:], tmpf[:], [C2], [CONST])
            for bi in range(4):
                add("pool", lambda e: e.memset(tmpf[:, 0:32], 0.0), [C2], [C2])
                add("pool", lambda e, bi=bi: e.affine_select(out=tmpf[:, 0:32], in_=tmpf[:, 0:32], pattern=[[1, 32]], compare_op=ALU.is_gt, fill=NEG, base=32 * bi, channel_multiplier=-1), [C2], [C2])
                add("pool", lambda e, bi=bi: e.affine_select(out=tmpf[:, 0:32], in_=tmpf[:, 0:32], pattern=[[0, 32]], compare_op=ALU.is_ge, fill=NEG, base=-32 * bi, channel_multiplier=1), [C2], [C2])
                add("pool", lambda e, bi=bi: e.affine_select(out=tmpf[:, 0:32], in_=tmpf[:, 0:32], pattern=[[0, 32]], compare_op=ALU.is_gt, fill=NEG, base=32 * bi + 16, channel_multiplier=-1), [C2], [C2])
                for h in range(8):
                    cp("dve", masks[:, bi, h, :], tmpf[:, 0:32], [C2], [CONST])
            add("pool", lambda e: e.memset(onesf[:], 1.0), [], [CONST])
            add("pool", lambda e: e.memset(bsps[:], 0.0), [], [CONST])
            for i, (t_, src) in enumerate([(gpre, g_pre_mix), (lng, ln_v_g), (lnb, ln_v_b), (goa, g_out_a), (gpost, g_post_mix), (gpf, g_pre_ffn), (gpostf, g_post_ffn)]):
                dma("sp", t_[:], src.partition_broadcast(128), [], [CONST], "cl%d" % i)
            add("sp", lambda e: e.dma_start(out=gob[:], in_=g_out_b.rearrange("(c p) -> p c", p=128), allow_slow_non_contiguous=True), [], [CONST], dmakey="cl7")
            add("sp", lambda e: e.dma_start(out=bsp[:], in_=b_spatial.rearrange("h t -> t h"), allow_slow_non_contiguous=True), [], [CONST], dmakey="cl8")
            for i in range(4):
                add("sp", lambda e, i=i: e.dma_start(out=bsps[32 * i:32 * i + 16, :], in_=b_spatial[:, 0:16].rearrange("h t -> t h"), allow_slow_non_contiguous=True), [CONST], [CONST], dmakey="cl9")
            WN = Buf("wn")
            for variant in range(2):
                if variant == 0:
                    dma("sp", wn[:], w_spatial.rearrange("h t s -> t h s"), [WN], [WN], "wn")
                else:
                    add("pool", lambda e: e.memset(wn[:], 0.0), [WN], [WN])
                    for i in range(4):
                        dma("sp", wn[32 * i:32 * i + 16, :, 32 * i:32 * i + 16], w_spatial[:, 0:16, 0:16].rearrange("h t s -> t h s"), [WN], [WN], "wn")
                for h in range(4):
                    add("pool", lambda e, h=h: e.affine_select(out=wn[:, h, :], in_=wn[:, h, :], pattern=[[-1, 128]], compare_op=ALU.is_ge, fill=0.0, base=0, channel_multiplier=1), [WN], [WN])
                cp("dve", wnb[:], wn[:], [WN], [WN])
                for h in range(4):
                    tr(pst[:, h, :], wnb[:, h, :], ident[:], [WN, CONST], [WN])
                cp("dve", (wsT if variant == 0 else wsTs)[:], pst[:], [WN], [WN, CONST])
            CS = Buf("cst")
            dma("sp", cst[0:3, :], conv_w, [], [CS], "cs0")
            dma("sp", cst[3:4, :], conv_b, [], [CS], "cs1")
            dma("sp", cst[4:12, :], cconv, [], [CS], "cs2")
            for c in range(44):
                pc = pc0 if c < 22 else pc1
                tr(pc[:, c % 22, :], cst[:, c * 128:(c + 1) * 128], identf[0:12, 0:12], [CS, CONST], [CS])
            cp("dve", cvec[:, 0:22, :], pc0[:], [CS], [CONST])
            cp("dve", cvec[:, 22:44, :], pc1[:], [CS], [CONST])
            wtmp = [sb(ph, "wtmp%d" % i, [128, 5632], BF16) for i in range(2)]
            WT = [Buf() for _ in range(2)]
            for c in range(8):
                dma("pool", wtmp[c % 2][:], w_up[c * 128:(c + 1) * 128, :], [], [WT[c % 2]], "wtl%d" % (c % 2))
                dma("sp", wups[c], wtmp[c % 2][:], [WT[c % 2]], [], "wts%d" % (c % 2))
            P.flush()

        with ExitStack() as st_ab:
            obT = sb(st_ab, "obT", [128, 4, NOWN * 128], BF16)
            NAT = [Buf("naT%d" % i) for i in range(NOWN)]
            OBT = Buf("obT")
            SSB = Buf("ssb")
            with ExitStack() as st_q:
                qT = sb(st_q, "qT", [128, 4, NOWN * 128], BF16)
                QT = [Buf("qT%d" % i) for i in range(NOWN)]
                kTn = sb(st_q, "kTn", [128, 4, 128], BF16)
                vnew = sb(st_q, "vnew", [128, 512], BF16)
                KTN = Buf("kTn")
                VNEW = Buf("vnew")
                KTS = Buf("kts")
                VSS = Buf("vss")

                if LEVEL < 1:
                    return nc
                with ExitStack() as ph:
                    win = sb(ph, "win", [128, 8, 2560], BF16)
                    WIN = Buf("win")
                    for c in range(8):
                        dma("pool", win[:, c, :], w_in[c * 128:(c + 1) * 128, :], [], [WIN], "win")
                    xt = [sb(ph, "xt%d" % i, [128, 1024], F32) for i in range(2)]
                    XT = [Buf() for _ in range(2)]
                    junk = sb(ph, "junk", [128, 1024], BF16)
                    JK = Buf()
                    st = sb(ph, "stat", [128, 16], F32)
                    ST = Buf()
                    hn = sb(ph, "hn", [128, 1024], BF16)
                    HN = Buf()
                    xT = [sb(ph, "xT%d" % i, [128, 8, 128], BF16) for i in range(2)]
                    XTT = [Buf() for _ in range(2)]
                    kst = [sb(ph, "kst%d" % i, [128, 4, 128], BF16) for i in range(2)]
                    KST = [Buf() for _ in range(2)]
                    vst = [sb(ph, "vst%d" % i, [128, 512], BF16) for i in range(2)]
                    VST = [Buf() for _ in range(2)]
                    vof = [sb(ph, "vof%d" % i, [128, 512], F32) for i in range(2)]
                    VOF = [Buf() for _ in range(2)]
                    kof = [sb(ph, "kof%d" % i, [128, 512], F32) for i in range(2)]
                    KOF = [Buf() for _ in range(2)]
                    za = sb(ph, "za", [128, 1024], F32)
                    ZA = Buf()
                    vn = sb(ph, "vn", [128, 512], F32)
                    VN = Buf()
                    va = [sb(ph, "va%d" % i, [128, 512], F32) for i in range(2)]
                    VA = [Buf() for _ in range(2)]
                    vab = sb(ph, "vab", [128, 512], BF16)
                    VAB = Buf()
                    outa = sb(ph, "outa", [128, 512], F32)
                    OA = Buf()
                    nab = sb(ph, "nab", [128, 512], BF16)
                    NAB = Buf()
                    nast = [sb(ph, "nast%d" % i, [128, 4, 128], BF16) for i in range(2)]
                    NAST = [Buf() for _ in range(2)]
                    ptr = ps(ph, "ptr", [128, 8, 128], BF16)
                    PTR = Buf()
                    pk = ps(ph, "pk", [128, 512])
                    PK = Buf()
                    pv = ps(ph, "pv", [128, 512])
                    PV = Buf()
                    pk2 = ps(ph, "pk2", [128, 512])
                    PK2 = Buf()
                    pq = ps(ph, "pq", [128, 512])
                    PQ = Buf()
                    puv = ps(ph, "puv", [128, 1024])
                    PUV = Buf()
                    psg = ps(ph, "psg", [128, 512])
                    PSG = Buf()

                    tiles = [("key", p, None) for p in range(HALO)]
                    tiles += [("own", p, p - HALO) for p in range(HALO, NPOS)]
                    tiles += [("samp", None, SAMP)]
                    tiles = [t_ for t_ in tiles if t_[0] in AKINDS]
                    for ti, (kind, pos, oi) in enumerate(tiles):
                        s = ti % 2
                        src = xs if kind == "samp" else xk[pos * 128:(pos + 1) * 128, :]
                        dma("sp", xt[s][:], src, [], [XT[s]], "xt%d" % s)
                        act(junk[:], xt[s][:], AF.Square, [XT[s]], [ST], accum_out=st[:, 0:1])
                        rstd(st[:, 1:2], st[:, 0:1], 1024, ST)
                        stt("dve", hn[:], xt[s][:], st[:, 1:2], gpre[:], ALU.mult, ALU.mult, [XT[s], ST, CONST], [HN])
                        for c in range(8):
                            tr(ptr[:, c, :], hn[:, c * 128:(c + 1) * 128], ident[:], [HN, CONST], [PTR])
                        cp("dve", xT[s][:], ptr[:], [PTR], [XTT[s]])
                        for fc in range(4):
                            for c in range(8):
                                mm(pk[:, fc * 128:(fc + 1) * 128], win[:, c, 1536 + fc * 128:1536 + (fc + 1) * 128], xT[s][:, c, :],
                                   fc == 0 and c == 0, fc == 3 and c == 7, [WIN, XTT[s]], [PK])
                        for c in range(8):
                            mm(pv[:], xT[s][:, c, :], win[:, c, 2048:2560], c == 0, c == 7, [WIN, XTT[s]], [PV])
                        if kind == "samp":
                            act(kTn[:].rearrange("p c t -> p (c t)"), pk[:], AF.Copy, [PK], [KTN], scale=0.125)
                        else:
                            act(kst[s][:].rearrange("p c t -> p (c t)"), pk[:], AF.Copy, [PK], [KST[s]], scale=0.125)
                            dma("pool", kts.rearrange("c p t -> p c t")[:, :, pos * 128:(pos + 1) * 128], kst[s][:], [KST[s]], [], "kst%d" % s)
                        if kind == "key":
                            cp("dve", vst[s][:], pv[:], [PV], [VST[s]])
                        else:
                            act(vof[s][:], pv[:], AF.Copy, [PV], [VOF[s]])
                            cp("dve", vst[s][:], vof[s][:], [VOF[s]], [VST[s]])
                        if kind == "samp":
                            cp("dve", vnew[:], vof[s][:], [VOF[s]], [VNEW])
                        else:
                            dma("pool", vss.rearrange("c p n d -> p c n d")[:, :, pos, :], vst[s][:].rearrange("p (c d) -> p c d", d=128), [VST[s]], [], "vst%d" % s)
                        if kind == "key":
                            continue
                        if ASTOP < 1:
                            continue
                        for c in range(8):
                            mm(pk2[:], xT[s][:, c, :], win[:, c, 1536:2048], c == 0, c == 7, [WIN, XTT[s]], [PK2])
                        act(kof[s][:], pk2[:], AF.Copy, [PK2], [KOF[s]])
                        if kind == "samp":
                            dma("pool", vso_o, vof[s][:], [VOF[s]], [], "vof%d" % s)
                            dma("pool", kso_o, kof[s][:], [KOF[s]], [], "kof%d" % s)
                        elif oi >= 1:
                            dma("pool", vo_o[(oi - 1) * 128:oi * 128, :], vof[s][:], [VOF[s]], [], "vof%d" % s)
                            dma("pool", ko_o[(oi - 1) * 128:oi * 128, :], kof[s][:], [KOF[s]], [], "kof%d" % s)
                        if ASTOP < 2:
                            continue
                        for fc in range(4):
                            for c in range(8):
                                mm(pq[:, fc * 128:(fc + 1) * 128], win[:, c, 1024 + fc * 128:1024 + (fc + 1) * 128], xT[s][:, c, :],
                                   fc == 0 and c == 0, fc == 3 and c == 7, [WIN, XTT[s]], [PQ])
                        cp("dve", qT[:, :, oi * 128:(oi + 1) * 128], pq[:].rearrange("p (c t) -> p c t", t=128), [PQ], [QT[oi]])
                        if ASTOP < 3:
                            continue
                        for hf in range(2):
                            for c in range(8):
                                mm(puv[:, hf * 512:(hf + 1) * 512], xT[s][:, c, :], win[:, c, hf * 512:(hf + 1) * 512], c == 0, c == 7, [WIN, XTT[s]], [PUV])
                        act(za[:], puv[:], AF.Gelu_apprx_tanh, [PUV], [ZA])
                        if ASTOP < 4:
                            continue
                        act(junk[:, 0:512], za[:, 512:1024], AF.Identity, [ZA], [ST], accum_out=st[:, 2:3])
                        act(junk[:, 512:1024], za[:, 512:1024], AF.Square, [ZA], [ST], accum_out=st[:, 3:4])
                        ts("dve", st[:, 4:5], st[:, 2:3], 1.0 / 512, None, ALU.mult, ALU.bypass, [ST], [ST])
                        tt("dve", st[:, 5:6], st[:, 4:5], st[:, 4:5], ALU.mult, [ST], [ST])
                        stt("dve", st[:, 6:7], st[:, 3:4], 1.0 / 512, st[:, 5:6], ALU.mult, ALU.subtract, [ST], [ST])
                        ts("dve", st[:, 6:7], st[:, 6:7], 1.0, EPS, ALU.mult, ALU.add, [ST], [ST])
                        act(st[:, 6:7], st[:, 6:7], AF.Sqrt, [ST], [ST])
                        add("dve", lambda e: e.reciprocal(out=st[:, 7:8], in_=st[:, 6:7]), [ST], [ST])
                        stt("dve", st[:, 8:9], st[:, 4:5], -1.0, st[:, 7:8], ALU.mult, ALU.mult, [ST], [ST])
                        act(vn[:], za[:, 512:1024], AF.Identity, [ZA, ST], [VN], scale=st[:, 7:8], bias=st[:, 8:9])
                        tt("dve", vn[:], vn[:], lng[:], ALU.mult, [VN, CONST], [VN])
                        tt("dve", va[s][:], vn[:], lnb[:], ALU.add, [VN, CONST], [VA[s]])
                        if kind == "samp":
                            dma("pool", vas_o, va[s][:], [VA[s]], [], "va%d" % s)
                        cp("pool", vab[:], va[s][:], [VA[s]], [VAB])
                        if ASTOP < 5:
                            continue
                        w_ = wsTs if kind == "samp" else wsT
                        b_ = bsps if kind == "samp" else bsp
                        for h in range(4):
                            mm(psg[:, h * 128:(h + 1) * 128], w_[:, h, :], vab[:, h * 128:(h + 1) * 128], h == 0, h == 3, [CONST, VAB], [PSG])
                        for h in range(4):
                            stt("dve", outa[:, h * 128:(h + 1) * 128], psg[:, h * 128:(h + 1) * 128], b_[:, h:h + 1], za[:, h * 128:(h + 1) * 128],
                                ALU.add, ALU.mult, [PSG, CONST, ZA], [OA])
                        if ASTOP < 6:
                            continue
                        act(junk[:, 0:512], outa[:], AF.Square, [OA], [ST], accum_out=st[:, 9:10])
                        rstd(st[:, 10:11], st[:, 9:10], 512, ST)
                        stt("dve", nab[:], outa[:], st[:, 10:11], goa[:], ALU.mult, ALU.mult, [OA, ST, CONST], [NAB])
                        for c in range(4):
                            tr(ptr[:, c, :], nab[:, c * 128:(c + 1) * 128], ident[:], [NAB, CONST], [PTR])
                        cp("dve", nast[s][:], ptr[:, 0:4, :], [PTR], [NAST[s]])
                        dma("pool", nas.rearrange("c p t -> p c t")[:, :, oi * 128:(oi + 1) * 128], nast[s][:], [NAST[s]], [], "nast%d" % s)
                    P.flush()

                if LEVEL < 2:
                    return nc
                with ExitStack() as ph:
                    ebuf = [sb(ph, "e%d" % i, [128, 2, 512], F32) for i in range(2)]
                    EB = [Buf() for _ in range(2)]
                    spb = [sb(ph, "sp%d" % i, [128, 2, 512], BF16) for i in range(2)]
                    SPB = [Buf() for _ in range(2)]
                    wtb = [sb(ph, "wt%d" % i, [128, 2, 512], BF16) for i in range(2)]
                    WTB = [Buf() for _ in range(2)]
                    srun = sb(ph, "srun", [128, 2, 512], BF16)
                    SR = Buf()
                    sqf = sb(ph, "sqf", [128, 512], BF16)
                    SQF = Buf()
                    pz = [ps(ph, "pz%d" % i, [128, 2, 512]) for i in range(2)]
                    PZ = [Buf() for _ in range(2)]
                    pa = ps(ph, "pa", [128, 2, 512])
                    PA = Buf()
                    po = ps(ph, "po", [128, 512])
                    PO = Buf()
                    pss = ps(ph, "pss", [128, NOWN, 2])
                    add("dve", lambda e: e.memset(pss[:], 0.0), [], [SSB])

                    def attention(steps, finish):
                        n = len(steps)

                        def zmm(dst, DST, stp):
                            banks = set()
                            for sg in stp["segs"]():
                                mm(sg["zout"](dst), sg["kT"], sg["q"], sg["bank"] not in banks, False, stp["reads"], [DST])
                                banks.add(sg["bank"])
                                if sg.get("mask") is not None:
                                    mm(sg["mout"](dst), sg["mlhs"], sg["mask"], False, False, [CONST], [DST])

                        def view(t, stp):
                            return stp["view"](t)

                        def pvmm(si, last):
                            pvs = steps[si]
                            if SKIPPV:
                                return
                            for (o_, l_, r_, first, tp) in pvs["pv"](po, wtb[si % 2]):
                                mm(o_, l_, r_, si == 0 and first, last, pvs["reads"] + [WTB[si % 2]], [PO], tile_position=tp)

                        add("dve", lambda e: e.memset(srun[:], 0.0), [], [SR])
                        zmm(pz[0], PZ[0], steps[0])
                        act(view(ebuf[0], steps[0]), view(pz[0], steps[0]), AF.Exp, [PZ[0]], [EB[0]])
                        act(view(spb[0], steps[0]), view(ebuf[0], steps[0]), AF.Ln, [EB[0]], [SPB[0]], bias=1.0)
                        for s_ in range(n):
                            b = s_ % 2
                            nb = (s_ + 1) % 2
                            stp = steps[s_]
                            if s_ + 1 < n:
                                nx = steps[s_ + 1]
                                zmm(pz[nb], PZ[nb], nx)
                                act(view(ebuf[nb], nx), view(pz[nb], nx), AF.Exp, [PZ[nb]], [EB[nb]])
                            zmm(pa, PA, stp)
                            if s_ > 0:
                                for (o_, l_, r_) in stp["carry"](pa, srun):
                                    mm(o_, l_, r_, False, False, [CONST, SR], [PA])
                            for (o_, l_, r_) in stp["cum"](pa, spb[b]):
                                mm(o_, l_, r_, False, False, [CONST, SPB[b]], [PA])
                            act(view(wtb[b], stp), view(pa, stp), AF.Exp, [PA], [WTB[b]])
                            if s_ + 1 < n:
                                act(view(spb[nb], nx), view(ebuf[nb], nx), AF.Ln, [EB[nb]], [SPB[nb]], bias=1.0)
                            tt("dve", view(srun, stp), view(srun, stp), view(spb[b], stp), ALU.add, [SR, SPB[b]], [SR])
                            if s_ > 0:
                                pvmm(s_ - 1, False)
                        pvmm(n - 1, True)
                        if SKIPPV < 2:
                            finish()

                    with ExitStack() as ph0:
                        ckb = sb(ph0, "ckb", [128, 8, 512], BF16)
                        CKB = Buf()
                        kTs = sb(ph0, "kTs", [128, 4, 2048], BF16)
                        KTSB = Buf()
                        vsb = sb(ph0, "vsb", [128, 16, 512], BF16)
                        VSB = Buf()
                        sqs = sb(ph0, "sqs", [128, 4, 128], BF16)
                        SQS = Buf()
                        ptk = po[:].bitcast(BF16).rearrange("p (j t) -> p j t", t=128)
                        for bi in range(4):
                            for half in range(2):
                                dma("pool", vsb[:, half * 8:(half + 1) * 8, :], cv[bi, half * 1024:(half + 1) * 1024, :].rearrange("(n p) f -> p n f", p=128), [], [VSB], "vsb")
                            for half in range(2):
                                dma("pool", ckb[:], ck[bi, half * 1024:(half + 1) * 1024, :].rearrange("(n p) f -> p n f", p=128), [], [CKB], "ckb")
                                for hp in range(4):
                                    for j in range(8):
                                        tr(ptk[:, j, :], ckb[:, j, hp * 128:(hp + 1) * 128], ident[:], [CKB, CONST], [PO])
                                    act(kTs[:, hp, half * 1024:(half + 1) * 1024], ptk.rearrange("p j t -> p (j t)"), AF.Copy, [PO], [KTSB], scale=0.125)
                            qc = SAMP * 128 + 32 * bi
                            steps = []
                            for kp in [16] + list(range(15, -1, -1)):
                                new = kp == 16
                                KPp = 128

                                def segs(new=new, kp=kp, KPp=KPp, qc=qc, bi=bi):
                                    out = []
                                    for h in range(8):
                                        hp, par = h // 2, h % 2
                                        r0 = 64 * par
                                        if new:
                                            kT_ = kTn[r0:r0 + 64, hp, :]
                                        else:
                                            kT_ = kTs[r0:r0 + 64, hp, kp * 128:(kp + 1) * 128]
                                        sg = {"kT": kT_, "q": qT[r0:r0 + 64, hp, qc:qc + 32], "bank": par,
                                              "zout": (lambda d, hp=hp, par=par: d[:, par, hp * 32:(hp + 1) * 32])}
                                        if new and h >= 6:
                                            sg["mask"] = masks[:, bi, 0:4].rearrange("p h q -> p (h q)")
                                            sg["mlhs"] = ident[:]
                                            sg["mout"] = (lambda d, par=par: d[:, par, 0:128])
                                        out.append(sg)
                                    return out

                                def pvf(po_, wt_, new=new, kp=kp, KPp=KPp, bi=bi):
                                    out = []
                                    for h in range(8):
                                        hp, par = h // 2, h % 2
                                        l_ = vnew[:, h * 64:(h + 1) * 64] if new else vsb[:, kp, h * 64:(h + 1) * 64]
                                        out.append((po_[64 * par:64 * par + 64, hp * 32:(hp + 1) * 32], l_, wt_[:, par, hp * 32:(hp + 1) * 32],
                                                    h < 2, (0, 64 * par)))
                                    return out

                                steps.append({
                                    "reads": [KTN, VNEW, QT[SAMP]] if new else [KTSB, VSB, QT[SAMP]],
                                    "view": (lambda t: t[:, :, 0:128]),
                                    "cum": (lambda pa_, sp_: [(pa_[:, par, 0:128], negT[:], sp_[:, par, 0:128]) for par in range(2)]),
                                    "carry": (lambda pa_, sr_: [(pa_[:, par, 0:128], negO[:], sr_[:, par, 0:128]) for par in range(2)]),
                                    "pv": pvf,
                                    "segs": segs,
                                })

                            def fin(bi=bi, qc=qc):
                                act(sqs[:, :, 32 * bi:32 * bi + 32], po[:, 0:128].rearrange("p (c q) -> p c q", q=32), AF.Square, [PO], [SQS])
                                for hp in range(4):
                                    act(obT[:, hp, qc:qc + 32], po[:, hp * 32:(hp + 1) * 32], AF.Identity, [PO, CONST], [OBT], scale=gob[:, hp:hp + 1])

                            if B0MODE >= 1:
                                attention(steps[:B0MODE], fin)
                        for hp in range(4):
                            mm(pss[:, SAMP, :], sqs[:, hp, :], onesf[:, 0:2], False, False, [SQS, CONST], [SSB])
                        P.flush()

                    if LEVEL < 3:
                        return nc
                    kTp = sb(ph, "kTp", [128, NPOS * 128], BF16)
                    KTP = Buf()
                    vp = sb(ph, "vp", [128, NPOS, 128], BF16)
                    VP = Buf()
                    groups = [[0]] + [[1 + 4 * g + j for j in range(4)] for g in range(NPR // 4)]
                    for hp in range(4):
                        dma("sp", kTp[:], kts[hp], [], [KTP], "kTp")
                        dma("sp", vp[:], vss[hp], [], [VP], "vp")
                        for grp in groups:
                            n = len(grp)
                            W = n * 128
                            p0 = HALO + grp[0]
                            qc = grp[0] * 128
                            steps = []
                            for kp in range(p0 + n - 1, -1, -1):
                                c0 = (kp - p0) * 128 if kp >= p0 else 0
                                diag = kp >= p0

                                def segs(kp=kp, c0=c0, diag=diag, W=W, qc=qc, hp=hp):
                                    out = []
                                    for par in range(2):
                                        r0 = 64 * par
                                        sg = {"kT": kTp[r0:r0 + 64, kp * 128:(kp + 1) * 128], "q": qT[r0:r0 + 64, hp, qc + c0:qc + W], "bank": par,
                                              "zout": (lambda d, par=par, c0=c0, W=W: d[:, par, c0:W])}
                                        if diag:
                                            sg["mask"] = maskb[:]
                                            sg["mlhs"] = ident[:]
                                            sg["mout"] = (lambda d, par=par, c0=c0: d[:, par, c0:c0 + 128])
                                        out.append(sg)
                                    return out

                                def pvf(po_, wt_, kp=kp, c0=c0, W=W):
                                    return [(po_[64 * par:64 * par + 64, c0:W], vp[:, kp, 64 * par:64 * par + 64], wt_[:, par, c0:W],
                                             True, (0, 64 * par)) for par in range(2)]

                                steps.append({
                                    "reads": [KTP, VP] + [QT[o] for o in grp],
                                    "view": (lambda t, c0=c0, W=W: t[:, :, c0:W]),
                                    "cum": (lambda pa_, sp_, c0=c0, W=W: [(pa_[:, par, c0:W], negT[:], sp_[:, par, c0:W]) for par in range(2)]),
                                    "carry": (lambda pa_, sr_, c0=c0, W=W: [(pa_[:, par, c0:W], negO[:], sr_[:, par, c0:W]) for par in range(2)]),
                                    "pv": pvf,
                                    "segs": segs,
                                })

                            def fin(grp=grp, W=W, qc=qc, hp=hp):
                                act(sqf[:, 0:W], po[:, 0:W], AF.Square, [PO], [SQF])
                                act(obT[:, hp, qc:qc + W], po[:, 0:W], AF.Identity, [PO, CONST], [OBT], scale=gob[:, hp:hp + 1])
                                for j, o in enumerate(grp):
                                    mm(pss[:, o, :], sqf[:, j * 128:(j + 1) * 128], onesf[:, 0:2], False, False, [SQF, CONST], [SSB])

                            attention(steps, fin)
                    cp("dve", ssb[:], pss[:, :, 0], [SSB], [SSB])
                    P.flush()
            if LEVEL < 4:
                return nc
            X1S = Buf("x1s")
            H2S = Buf("h2s")
            with ExitStack() as ph:
                wo = sb(ph, "wo", [128, 8, 1024], BF16)
                WO = Buf()
                for c in range(8):
                    dma("pool", wo[:, c, :], w_out[c * 128:(c + 1) * 128, :], [], [WO], "wo")
                nat = [sb(ph, "nat%d" % i, [128, 4, 128], BF16) for i in range(2)]
                NATB = [Buf() for _ in range(2)]
                xt = [sb(ph, "cxt%d" % i, [128, 1024], F32) for i in range(2)]
                XT = [Buf() for _ in range(2)]
                mixa = sb(ph, "mixa", [128, 1024], F32)
                MA = Buf()
                mix = sb(ph, "mix", [128, 1024], F32)
                MX = Buf()
                x1 = [sb(ph, "x1_%d" % i, [128, 1024], F32) for i in range(2)]
                X1 = [Buf() for _ in range(2)]
                junk = sb(ph, "cjunk", [128, 1024], BF16)
                JK = Buf()
                st = sb(ph, "cstat", [128, 8], F32)
                ST = Buf()
                hn2 = sb(ph, "hn2", [128, 1024], BF16)
                HN2 = Buf()
                h2st = [sb(ph, "h2st%d" % i, [128, 8, 128], BF16) for i in range(2)]
                H2ST = [Buf() for _ in range(2)]
                pma = ps(ph, "pma", [128, 1024])
                PMA = Buf()
                pmb = ps(ph, "pmb", [128, 1024])
                PMB = Buf()
                ptr = ps(ph, "cptr", [128, 8, 128], BF16)
                PTR = Buf()
                for oi in range(NOWN):
                    s = oi % 2
                    src = xs if oi == SAMP else xk[(HALO + oi) * 128:(HALO + oi + 1) * 128, :]
                    dma("sp", xt[s][:], src, [], [XT[s]], "cxt%d" % s)
                    cols = slice(oi * 128, (oi + 1) * 128)
                    dma("sp", nat[s][:], nas.rearrange("c p t -> p c t")[:, :, cols], [], [NATB[s]], "nat%d" % s)
                    for hf in range(2):
                        for c in range(4):
                            mm(pma[:, hf * 512:(hf + 1) * 512], nat[s][:, c, :], wo[:, c, hf * 512:(hf + 1) * 512], c == 0, c == 3, [NATB[s], WO], [PMA])
                    for hf in range(2):
                        for c in range(4):
                            mm(pmb[:, hf * 512:(hf + 1) * 512], obT[:, c, cols], wo[:, 4 + c, hf * 512:(hf + 1) * 512], c == 0, c == 3, [OBT, WO], [PMB])
                    rstd(st[:, 0:1], ssb[:, oi:oi + 1], 512, ST if oi else ST)
                    act(mixa[:], pma[:], AF.Copy, [PMA], [MA])
                    stt("dve", mix[:], pmb[:], st[:, 0:1], mixa[:], ALU.mult, ALU.add, [PMB, ST, MA, SSB], [MX])
                    act(junk[:], mix[:], AF.Square, [MX], [ST], accum_out=st[:, 1:2])
                    rstd(st[:, 2:3], st[:, 1:2], 1024, ST)
                    stt("dve", mix[:], mix[:], st[:, 2:3], gpost[:], ALU.mult, ALU.mult, [MX, ST, CONST], [MX])
                    tt("dve", x1[s][:], mix[:], xt[s][:], ALU.add, [MX, XT[s]], [X1[s]])
                    dma("pool", x1s[oi], x1[s][:], [X1[s]], [], "x1st%d" % s)
                    act(junk[:], x1[s][:], AF.Square, [X1[s]], [ST], accum_out=st[:, 3:4])
                    rstd(st[:, 4:5], st[:, 3:4], 1024, ST)
                    stt("dve", hn2[:], x1[s][:], st[:, 4:5], gpf[:], ALU.mult, ALU.mult, [X1[s], ST, CONST], [HN2])
                    for c in range(8):
                        tr(ptr[:, c, :], hn2[:, c * 128:(c + 1) * 128], ident[:], [HN2, CONST], [PTR])
                    cp("dve", h2st[s][:], ptr[:], [PTR], [H2ST[s]])
                    dma("pool", h2s.rearrange("c p t -> p c t")[:, :, oi * 128:(oi + 1) * 128], h2st[s][:], [H2ST[s]], [], "h2st%d" % s)
                P.flush()
        if LEVEL < 5:
            return nc
        with ExitStack() as ph:
            wd = sb(ph, "wd", [128, 22, 1024], BF16)
            WD = Buf()
            for c in range(22):
                dma("pool", wd[:, c, :], w_down[c * 128:(c + 1) * 128, :], [], [WD], "wd")
            hg = [sb(ph, "hg%d" % i, [128, 8, 512], BF16) for i in range(2)]
            HG = [Buf() for _ in range(2)]
            wuf = [sb(ph, "wuf%d" % i, [128, 2, 8, 128], BF16) for i in range(3)]
            WUF = [Buf() for _ in range(3)]
            ug = [sb(ph, "ug%d" % i, [128, 514], F32) for i in range(2)]
            UG = [Buf() for _ in range(2)]
            uv = [sb(ph, "uv%d" % i, [128, 514], F32) for i in range(2)]
            UV = [Buf() for _ in range(2)]
            yg = [sb(ph, "yg%d" % i, [128, 512], F32) for i in range(2)]
            YG = [Buf() for _ in range(2)]
            yv = [sb(ph, "yv%d" % i, [128, 512], F32) for i in range(2)]
            YV = [Buf() for _ in range(2)]
            gl = [sb(ph, "gl%d" % i, [128, 512], F32) for i in range(2)]
            GL = [Buf() for _ in range(2)]
            aT = sb(ph, "aT", [128, 22, 512], BF16)
            AT = Buf()
            HIST = Buf()
            x1t = [sb(ph, "x1t%d" % i, [128, 1024], F32) for i in range(2)]
            X1T = [Buf() for _ in range(2)]
            yo = [sb(ph, "yo%d" % i, [128, 1024], F32) for i in range(2)]
            YO = [Buf() for _ in range(2)]
            tmst = [sb(ph, "tmst%d" % i, [128, 256], F32) for i in range(2)]
            TMST = [Buf() for _ in range(2)]
            junk = sb(ph, "djunk", [128, 1024], BF16)
            JK = Buf()
            st = sb(ph, "dstat", [128, 4], F32)
            ST = Buf()
            pg = [ps(ph, "pg%d" % i, [128, 512]) for i in range(2)]
            PG = [Buf() for _ in range(2)]
            pvv = [ps(ph, "pvv%d" % i, [128, 512]) for i in range(2)]
            PVV = [Buf() for _ in range(2)]
            pf = ps(ph, "pf", [128, 1024])
            PF = Buf()
            pt = ps(ph, "pt", [128, 256])
            PT = Buf()
            h2v = h2s.rearrange("c p t -> p c t")
            cnt = {"g": 0, "f": 0, "t": 0, "x": 0}

            def load_wuf(fc):
                s = cnt["f"] % 3
                cnt["f"] += 1
                wv = wups.rearrange("c p n -> p c n")
                dma("sp", wuf[s][:, 0], wv[:, :, fc * 128:(fc + 1) * 128], [], [WUF[s]], "wuf%d" % s)
                dma("sp", wuf[s][:, 1], wv[:, :, (22 + fc) * 128:(23 + fc) * 128], [], [WUF[s]], "wuf%d" % s)
                return s

            gs = cnt["g"] % 2
            cnt["g"] += 1
            dma("sp", hg[gs][:, :, 0:2], h2v[:, :, 126:128], [], [HG[gs]], "hg%d" % gs)
            for fc in range(22):
                ws = load_wuf(fc)
                for typ in range(2):
                    ch = fc + 22 * typ
                    for c in range(8):
                        mm(pg[0][:, ch * 2:ch * 2 + 2], wuf[ws][:, typ, c, :], hg[gs][:, c, 0:2], fc == 0 and typ == 0 and c == 0, fc == 21 and typ == 1 and c == 7,
                           [WUF[ws], HG[gs]], [PG[0]])
            cp("dve", hist[:], pg[0][:, 0:88].rearrange("p (f t) -> p f t", t=2), [PG[0]], [HIST])

            def ffn_group(ois, sample, tm_tile, out_fn):
                n = len(ois)
                W = n * 128
                gs = cnt["g"] % 2
                cnt["g"] += 1
                dma("sp", hg[gs][:, :, 0:W], h2v[:, :, ois[0] * 128:ois[0] * 128 + W], [], [HG[gs]], "hg%d" % gs)
                for fc in range(22):
                    ws = load_wuf(fc)
                    b = fc % 2
                    for typ, (pp, PP, uu, UU, yy, YY) in enumerate([(pg[b], PG[b], ug[b], UG[b], yg[b], YG[b]), (pvv[b], PVV[b], uv[b], UV[b], yv[b], YV[b])]):
                        ch = fc + 22 * typ
                        for c in range(8):
                            mm(pp[:, 0:W], wuf[ws][:, typ, c, :], hg[gs][:, c, 0:W], c == 0, c == 7, [WUF[ws], HG[gs]], [PP])
                        act(uu[:, 2:2 + W], pp[:, 0:W], AF.Copy, [PP], [UU])
                        act(yy[:, 0:W], pp[:, 0:W], AF.Identity, [PP, CONST], [YY], scale=cvec[:, ch, 2:3], bias=cvec[:, ch, 3:4])
                        if sample:
                            cp("pool", uu[:, 0:128].rearrange("p (i c) -> p i c", c=32)[:, :, 0:2], cvec[:, ch, 4:12].rearrange("p (i c) -> p i c", c=2), [CONST, UU], [UU])
                        else:
                            cp("pool", uu[:, 0:2], hist[:, ch, :], [HIST, UU], [UU])
                            cp("pool", hist[:, ch, :], uu[:, W:W + 2], [UU, HIST], [HIST])
                        stt("dve", yy[:, 0:W], uu[:, 1:1 + W], cvec[:, ch, 1:2], yy[:, 0:W], ALU.mult, ALU.add, [UU, CONST, YY], [YY])
                        stt("dve", yy[:, 0:W], uu[:, 0:W], cvec[:, ch, 0:1], yy[:, 0:W], ALU.mult, ALU.add, [UU, CONST, YY], [YY])
                    act(gl[b][:, 0:W], yg[b][:, 0:W], AF.Gelu_apprx_tanh, [YG[b]], [GL[b]])
                    tt("pool", aT[:, fc, 0:W], gl[b][:, 0:W], yv[b][:, 0:W], ALU.mult, [GL[b], YV[b]], [AT])
                    if tm_tile is not None:
                        j, dst = tm_tile
                        ts_ = cnt["t"] % 2
                        cnt["t"] += 1
                        for typ in range(2):
                            for c in range(8):
                                mm(pt[:, typ * 128:(typ + 1) * 128], hg[gs][:, c, j * 128:(j + 1) * 128], wuf[ws][:, typ, c, :],
                                   typ == 0 and c == 0, typ == 1 and c == 7, [WUF[ws], HG[gs]], [PT])
                        act(tmst[ts_][:], pt[:], AF.Copy, [PT], [TMST[ts_]])
                        dma("pool", dst[:, fc * 128:(fc + 1) * 128], tmst[ts_][:, 0:128], [TMST[ts_]], [], "tmst%d" % ts_)
                        dma("pool", dst[:, (22 + fc) * 128:(23 + fc) * 128], tmst[ts_][:, 128:256], [TMST[ts_]], [], "tmst%d" % ts_)
                for j, oi in enumerate(ois):
                    xs_ = cnt["x"] % 2
                    cnt["x"] += 1
                    dma("sp", x1t[xs_][:], x1s[oi], [], [X1T[xs_]], "x1t%d" % xs_)
                    for hf in range(2):
                        for fc in range(22):
                            mm(pf[:, hf * 512:(hf + 1) * 512], aT[:, fc, j * 128:(j + 1) * 128], wd[:, fc, hf * 512:(hf + 1) * 512], fc == 0, fc == 21, [AT, WD], [PF])
                    act(junk[:], pf[:], AF.Square, [PF], [ST], accum_out=st[:, 0:1])
                    rstd(st[:, 1:2], st[:, 0:1], 1024, ST)
                    stt("dve", yo[xs_][:], pf[:], st[:, 1:2], gpostf[:], ALU.mult, ALU.mult, [PF, ST, CONST], [YO[xs_]])
                    tt("dve", yo[xs_][:], yo[xs_][:], x1t[xs_][:], ALU.add, [YO[xs_], X1T[xs_]], [YO[xs_]])
                    dma("pool", out_fn(oi), yo[xs_][:], [YO[xs_]], [], "yo%d" % xs_)

            for g in range(NPR // 4):
                ois = [1 + 4 * g + j for j in range(4)]
                ffn_group(ois, False, (3, upl_o) if g == NPR // 4 - 1 else None, lambda oi: y_o[(oi - 1) * 128:oi * 128, :])
            ffn_group([SAMP], True, (0, ups_o), lambda oi: ys_o)
            P.flush()
      except _Stop:
        pass
    return nc


_NC = None


def kernel(**inputs):
    global _NC
    f = lambda k: np.ascontiguousarray(np.asarray(inputs[k], dtype=np.float32))
    xp, xsm = f("x_prompt"), f("x_sample")
    ckf = f("cache_sb_k")[0].reshape(32, 2048, 512)
    cvf = f("cache_sb_v")[0].reshape(32, 2048, 512)
    ccf = f("cache_ffn_conv")[0]
    shared = {
        "w_in": f("w_in")[0], "w_out": f("w_out")[0], "w_up": f("w_up")[0], "w_down": f("w_down")[0],
        "g_pre_mix": f("g_pre_mix")[0], "ln_v_g": f("ln_v_g")[0], "ln_v_b": f("ln_v_b")[0],
        "w_spatial": f("w_spatial")[0], "b_spatial": f("b_spatial")[0], "g_out_a": f("g_out_a")[0],
        "g_out_b": f("g_out_b")[0], "g_post_mix": f("g_post_mix")[0], "g_pre_ffn": f("g_pre_ffn")[0],
        "conv_w": f("conv_w")[0], "conv_b": f("conv_b"), "g_post_ffn": f("g_post_ffn")[0],
    }
    in_maps = []
    for c in range(8):
        b, half = c // 2, c % 2
        xk = np.zeros((NPOS * 128, 1024), np.float32)
        if half == 0:
            xk[4096:] = xp[b, 0:4096]
        else:
            xk[:] = xp[b]
        xs = np.zeros((128, 1024), np.float32)
        for i in range(4):
            xs[32 * i:32 * i + 16] = xsm[4 * c + i]
        m = dict(shared)
        m.update({"xk": xk, "xs": xs, "ck": np.ascontiguousarray(ckf[4 * c:4 * c + 4]), "cv": np.ascontiguousarray(cvf[4 * c:4 * c + 4]),
                  "cconv": np.ascontiguousarray(ccf[4 * c:4 * c + 4].reshape(8, 5632))})
        in_maps.append(m)
    if _NC is None:
        _NC = build()
    res = run_bass_kernel_spmd(_NC, in_maps, core_ids=list(range(8)))
    R = res.results
    yp = np.zeros((4, 8192, 1024), np.float32)
    ys = np.zeros((32, 16, 1024), np.float32)
    kp = np.zeros((1, 4, 8192, 8, 64), np.float32)
    vp = np.zeros((1, 4, 8192, 8, 64), np.float32)
    ksn = np.zeros((1, 32, 16, 8, 64), np.float32)
    vsn = np.zeros((1, 32, 16, 8, 64), np.float32)
    vas = np.zeros((1, 32, 16, 512), np.float32)
    cpp = np.zeros((1, 4, 2, 5632), np.float32)
    css = np.zeros((1, 32, 2, 5632), np.float32)
    for c in range(8):
        b, half = c // 2, c % 2
        r = R[c]
        sl = slice(half * 4096, (half + 1) * 4096)
        yp[b, sl] = r["y"]
        kp[0, b, sl] = r["ko"].reshape(4096, 8, 64)
        vp[0, b, sl] = r["vo"].reshape(4096, 8, 64)
        if half == 1:
            cpp[0, b] = r["upl"][126:128]
        for i in range(4):
            rows = slice(32 * i, 32 * i + 16)
            ys[4 * c + i] = r["ys"][rows]
            ksn[0, 4 * c + i] = r["kso"][rows].reshape(16, 8, 64)
            vsn[0, 4 * c + i] = r["vso"][rows].reshape(16, 8, 64)
            vas[0, 4 * c + i] = r["vas"][rows]
            css[0, 4 * c + i] = r["ups"][32 * i + 14:32 * i + 16]
    return (yp, ys, kp, vp, ksn, vsn, vas, cpp, css)
```

```python
import numpy as np
from contextlib import ExitStack
import concourse.bass as bass
import concourse.mybir as mybir
from concourse.bass_utils import run_bass_kernel_spmd

F32 = mybir.dt.float32
BF16 = mybir.dt.bfloat16
AF = mybir.ActivationFunctionType
ALU = mybir.AluOpType
ENGS = ("sp", "act", "pool", "dve", "pe")
SEM_EPOCH = 8000

NPOS = 64
HALO = 31
NOWN = 34
SAMP = 33
EPS = 1e-6
NEG = -30000.0
LEVEL = 9
AKINDS = ("key", "own", "samp")
ASTOP = 99
B0MODE = 9
SKIPPV = 0


class _Stop(Exception):
    pass


class Buf:
    __slots__ = ("name", "w", "r")

    def __init__(self, name=""):
        self.name = name
        self.w = None
        self.r = []


class Op:
    __slots__ = ("id", "eng", "fn", "deps", "dmakey", "sem", "val", "awaited")


class Prog:
    def __init__(self, nc, es):
        self.nc = nc
        self.es = es
        self.ops = []
        self.start = 0
        self.dmasem = {}
        self.nsem = 0

    def add(self, eng, fn, reads=(), writes=(), dmakey=None):
        op = Op()
        op.id = len(self.ops)
        op.eng = eng
        op.fn = fn
        deps = set()
        for b in reads:
            if b.w is not None:
                deps.add(b.w)
        for b in writes:
            if b.w is not None:
                deps.add(b.w)
            deps.update(b.r)
        op.deps = {d for d in deps if d >= self.start}
        op.dmakey = dmakey
        op.sem = None
        op.val = 0
        op.awaited = False
        for b in reads:
            b.r.append(op.id)
        for b in writes:
            b.w = op.id
            b.r = []
        self.ops.append(op)
        return op

    def flush(self):
        nc = self.nc
        ops = self.ops
        cur = ops[self.start:]
        if not cur:
            return
        for op in cur:
            for d in op.deps:
                dop = ops[d]
                if dop.eng == "pe" and op.eng == "pe" and dop.dmakey is None:
                    continue
                dop.awaited = True
        per = {e: [o for o in cur if o.eng == e] for e in ENGS}
        last = {}
        for e in ENGS:
            comp = [o for o in per[e] if o.dmakey is None and o.fn is not None]
            if comp:
                comp[-1].awaited = True
                last[e] = comp[-1]
        for e in ENGS:
            n = 0
            sem = None
            for op in per[e]:
                if op.fn is None:
                    continue
                if op.dmakey is not None:
                    if op.dmakey not in self.dmasem:
                        self.nsem += 1
                        self.dmasem[op.dmakey] = [self.es.enter_context(nc.semaphore("sd%d" % self.nsem)), 0]
                    ent = self.dmasem[op.dmakey]
                    ent[1] += 16
                    op.sem, op.val = ent[0], ent[1]
                elif op.awaited:
                    if sem is None or n >= SEM_EPOCH:
                        self.nsem += 1
                        sem = self.es.enter_context(nc.semaphore("se%d" % self.nsem))
                        n = 0
                    n += 1
                    op.sem, op.val = sem, n
        barrier = [(o.sem, o.val) for o in last.values()]
        barrier += [(s, v) for (s, v) in self.dmasem.values()]

        def run(ename, e):
            seen = {}

            def wait(sem, val):
                k = id(sem)
                if seen.get(k, 0) >= val:
                    return
                seen[k] = val
                e.wait_ge(sem, val)

            for op in per[ename]:
                need = {}
                for d in sorted(op.deps):
                    dop = ops[d]
                    if dop.sem is None:
                        continue
                    if dop.eng == "pe" and ename == "pe" and dop.dmakey is None:
                        continue
                    k = id(dop.sem)
                    if k not in need or need[k][1] < dop.val:
                        need[k] = (dop.sem, dop.val)
                for (s_, v_) in need.values():
                    wait(s_, v_)
                if op.fn is None:
                    continue
                ins = op.fn(e)
                if op.dmakey is not None:
                    ins.then_inc(op.sem, 16)
                elif op.awaited:
                    ins.then_inc(op.sem, 1)
            for (s, v) in barrier:
                wait(s, v)

        with nc.Block() as block:
            @block.sync
            def _(e):
                run("sp", e)

            @block.scalar
            def _(e):
                run("act", e)

            @block.gpsimd
            def _(e):
                run("pool", e)

            @block.vector
            def _(e):
                run("dve", e)

            @block.tensor
            def _(e):
                run("pe", e)
        self.start = len(ops)


def build():
    nc = bass.Bass("TRN2", target_bir_lowering=False)
    di = lambda n, s: nc.dram_tensor(n, list(s), F32, kind="ExternalInput").ap()
    do = lambda n, s: nc.dram_tensor(n, list(s), F32, kind="ExternalOutput").ap()
    dscr = lambda n, s, dt: nc.dram_tensor(n, list(s), dt, kind="Internal").ap()

    xk = di("xk", [NPOS * 128, 1024])
    xs = di("xs", [128, 1024])
    ck = di("ck", [4, 2048, 512])
    cv = di("cv", [4, 2048, 512])
    cconv = di("cconv", [8, 5632])
    w_in = di("w_in", [1024, 2560])
    w_out = di("w_out", [1024, 1024])
    w_up = di("w_up", [1024, 5632])
    w_down = di("w_down", [2816, 1024])
    g_pre_mix = di("g_pre_mix", [1024])
    ln_v_g = di("ln_v_g", [512])
    ln_v_b = di("ln_v_b", [512])
    w_spatial = di("w_spatial", [4, 128, 128])
    b_spatial = di("b_spatial", [4, 128])
    g_out_a = di("g_out_a", [512])
    g_out_b = di("g_out_b", [512])
    g_post_mix = di("g_post_mix", [1024])
    g_pre_ffn = di("g_pre_ffn", [1024])
    conv_w = di("conv_w", [3, 5632])
    conv_b = di("conv_b", [1, 5632])
    g_post_ffn = di("g_post_ffn", [1024])

    NPR = NOWN - 2
    y_o = do("y", [NPR * 128, 1024])
    ys_o = do("ys", [128, 1024])
    ko_o = do("ko", [NPR * 128, 512])
    vo_o = do("vo", [NPR * 128, 512])
    kso_o = do("kso", [128, 512])
    vso_o = do("vso", [128, 512])
    vas_o = do("vas", [128, 512])
    upl_o = do("upl", [128, 5632])
    ups_o = do("ups", [128, 5632])

    kts = dscr("kts", [4, 128, NPOS * 128], BF16)
    vss = dscr("vss", [4, 128, NPOS, 128], BF16)
    x1s = dscr("x1s", [NOWN, 128, 1024], F32)
    h2s = dscr("h2s", [8, 128, NOWN * 128], BF16)
    wups = dscr("wups", [8, 128, 5632], BF16)
    nas = dscr("nas", [4, 128, NOWN * 128], BF16)

    with ExitStack() as es0:
      es = es0.enter_context(ExitStack())
      try:
        P = Prog(nc, es)
        add = P.add
        OUT = Buf("outs")

        def sb(st, n, s, d):
            return st.enter_context(nc.sbuf_tensor(n, list(s), d))

        def ps(st, n, s, d=F32):
            return st.enter_context(nc.psum_tensor(n, list(s), d))

        def dma(q, out, in_, reads, writes, key):
            add(q, lambda e: e.dma_start(out=out, in_=in_), reads, writes, dmakey=key)

        def act(out, in_, func, reads, writes, **kw):
            add("act", lambda e: e.activation(out=out, in_=in_, func=func, **kw), reads, writes)

        def mm(out, lhsT, rhs, start, stop, reads, writes, tile_position=None):
            if tile_position is None:
                add("pe", lambda e: e.matmul(out, lhsT=lhsT, rhs=rhs, start=start, stop=stop, skip_group_check=True), reads, writes)
            else:
                add("pe", lambda e: e.matmul(out, lhsT=lhsT, rhs=rhs, start=start, stop=stop, skip_group_check=True, tile_position=tile_position), reads, writes)

        def tr(out, in_, ident, reads, writes):
            add("pe", lambda e: e.transpose(out=out, in_=in_, identity=ident), reads, writes)

        def stt(eng, out, in0, scalar, in1, op0, op1, reads, writes):
            add(eng, lambda e: e.scalar_tensor_tensor(out=out, in0=in0, scalar=scalar, in1=in1, op0=op0, op1=op1), reads, writes)

        def ts(eng, out, in0, s1, s2, op0, op1, reads, writes):
            if s2 is None:
                add(eng, lambda e: e.tensor_scalar(out=out, in0=in0, scalar1=s1, scalar2=None, op0=op0), reads, writes)
            else:
                add(eng, lambda e: e.tensor_scalar(out=out, in0=in0, scalar1=s1, scalar2=s2, op0=op0, op1=op1), reads, writes)

        def tt(eng, out, in0, in1, op, reads, writes):
            add(eng, lambda e: e.tensor_tensor(out=out, in0=in0, in1=in1, op=op), reads, writes)

        def cp(eng, out, in_, reads, writes):
            add(eng, lambda e: e.tensor_copy(out=out, in_=in_), reads, writes)

        def rstd(r, ssum, n, b):
            ts("dve", r, ssum, 1.0 / n, EPS, ALU.mult, ALU.add, [b], [b])
            act(r, r, AF.Sqrt, [b], [b])
            add("dve", lambda e: e.reciprocal(out=r, in_=r), [b], [b])

        ident = sb(es, "ident", [128, 128], BF16)
        identf = sb(es, "identf", [128, 128], F32)
        negT = sb(es, "negT", [128, 128], BF16)
        negO = sb(es, "negO", [128, 128], BF16)
        maskb = sb(es, "maskb", [128, 128], BF16)
        masks = sb(es, "masks", [128, 4, 8, 32], BF16)
        onesf = sb(es, "onesf", [128, 2], BF16)
        tmpf = sb(es, "tmpf", [128, 128], F32)
        gpre = sb(es, "gpre", [128, 1024], F32)
        lng = sb(es, "lng", [128, 512], F32)
        lnb = sb(es, "lnb", [128, 512], F32)
        goa = sb(es, "goa", [128, 512], F32)
        gpost = sb(es, "gpost", [128, 1024], F32)
        gpf = sb(es, "gpf", [128, 1024], F32)
        gpostf = sb(es, "gpostf", [128, 1024], F32)
        gob = sb(es, "gob", [128, 4], F32)
        bsp = sb(es, "bsp", [128, 4], F32)
        bsps = sb(es, "bsps", [128, 4], F32)
        wsT = sb(es, "wsT", [128, 4, 128], BF16)
        wsTs = sb(es, "wsTs", [128, 4, 128], BF16)
        cvec = sb(es, "cvec", [128, 44, 12], F32)
        hist = sb(es, "hist", [128, 44, 2], F32)
        ssb = sb(es, "ssb", [128, NOWN], F32)
        CONST = Buf("const")

        with ExitStack() as ph:
            wn = sb(ph, "wn", [128, 4, 128], F32)
            wnb = sb(ph, "wnb", [128, 4, 128], BF16)
            cst = sb(ph, "cst", [12, 5632], F32)
            pst = ps(ph, "pst", [128, 4, 128], BF16)
            pc0 = ps(ph, "pc0", [128, 22, 12], F32)
            pc1 = ps(ph, "pc1", [128, 22, 12], F32)
            C2 = Buf("c2")
            add("pool", lambda e: e.memset(identf[:], 0.0), [], [CONST])
            add("pool", lambda e: e.affine_select(out=identf[:], in_=identf[:], pattern=[[-1, 128]], compare_op=ALU.not_equal, fill=1.0, base=0, channel_multiplier=1), [CONST], [CONST])
            cp("dve", ident[:], identf[:], [CONST], [CONST])
            add("pool", lambda e: e.memset(tmpf[:], -1.0), [], [C2])
            cp("dve", negO[:], tmpf[:], [C2], [CONST])
            add("pool", lambda e: e.affine_select(out=tmpf[:], in_=tmpf[:], pattern=[[-1, 128]], compare_op=ALU.is_ge, fill=0.0, base=0, channel_multiplier=1), [C2], [C2])
            cp("dve", negT[:], tmpf[:], [C2], [CONST])
            add("pool", lambda e: e.memset(tmpf[:], 0.0), [C2], [C2])
            add("pool", lambda e: e.affine_select(out=tmpf[:], in_=tmpf[:], pattern=[[1, 128]], compare_op=ALU.is_gt, fill=NEG, base=0, channel_multiplier=-1), [C2], [C2])
            cp("dve", maskb[:], tmpf[:], [C2], [CONST])
            for bi in range(4):
                add("pool", lambda e: e.memset(tmpf[:, 0:32], 0.0), [C2], [C2])
                add("pool", lambda e, bi=bi: e.affine_select(out=tmpf[:, 0:32], in_=tmpf[:, 0:32], pattern=[[1, 32]], compare_op=ALU.is_gt, fill=NEG, base=32 * bi, channel_multiplier=-1), [C2], [C2])
                add("pool", lambda e, bi=bi: e.affine_select(out=tmpf[:, 0:32], in_=tmpf[:, 0:32], pattern=[[0, 32]], compare_op=ALU.is_ge, fill=NEG, base=-32 * bi, channel_multiplier=1), [C2], [C2])
                add("pool", lambda e, bi=bi: e.affine_select(out=tmpf[:, 0:32], in_=tmpf[:, 0:32], pattern=[[0, 32]], compare_op=ALU.is_gt, fill=NEG, base=32 * bi + 16, channel_multiplier=-1), [C2], [C2])
                for h in range(8):
                    cp("dve", masks[:, bi, h, :], tmpf[:, 0:32], [C2], [CONST])
            add("pool", lambda e: e.memset(onesf[:], 1.0), [], [CONST])
            add("pool", lambda e: e.memset(bsps[:], 0.0), [], [CONST])
            for i, (t_, src) in enumerate([(gpre, g_pre_mix), (lng, ln_v_g), (lnb, ln_v_b), (goa, g_out_a), (gpost, g_post_mix), (gpf, g_pre_ffn), (gpostf, g_post_ffn)]):
                dma("sp", t_[:], src.partition_broadcast(128), [], [CONST], "cl%d" % i)
            add("sp", lambda e: e.dma_start(out=gob[:], in_=g_out_b.rearrange("(c p) -> p c", p=128), allow_slow_non_contiguous=True), [], [CONST], dmakey="cl7")
            add("sp", lambda e: e.dma_start(out=bsp[:], in_=b_spatial.rearrange("h t -> t h"), allow_slow_non_contiguous=True), [], [CONST], dmakey="cl8")
            for i in range(4):
                add("sp", lambda e, i=i: e.dma_start(out=bsps[32 * i:32 * i + 16, :], in_=b_spatial[:, 0:16].rearrange("h t -> t h"), allow_slow_non_contiguous=True), [CONST], [CONST], dmakey="cl9")
            WN = Buf("wn")
            for variant in range(2):
                if variant == 0:
                    dma("sp", wn[:], w_spatial.rearrange("h t s -> t h s"), [WN], [WN], "wn")
                else:
                    add("pool", lambda e: e.memset(wn[:], 0.0), [WN], [WN])
                    for i in range(4):
                        dma("sp", wn[32 * i:32 * i + 16, :, 32 * i:32 * i + 16], w_spatial[:, 0:16, 0:16].rearrange("h t s -> t h s"), [WN], [WN], "wn")
                for h in range(4):
                    add("pool", lambda e, h=h: e.affine_select(out=wn[:, h, :], in_=wn[:, h, :], pattern=[[-1, 128]], compare_op=ALU.is_ge, fill=0.0, base=0, channel_multiplier=1), [WN], [WN])
                cp("dve", wnb[:], wn[:], [WN], [WN])
                for h in range(4):
                    tr(pst[:, h, :], wnb[:, h, :], ident[:], [WN, CONST], [WN])
                cp("dve", (wsT if variant == 0 else wsTs)[:], pst[:], [WN], [WN, CONST])
            CS = Buf("cst")
            dma("sp", cst[0:3, :], conv_w, [], [CS], "cs0")
            dma("sp", cst[3:4, :], conv_b, [], [CS], "cs1")
            dma("sp", cst[4:12, :], cconv, [], [CS], "cs2")
            for c in range(44):
                pc = pc0 if c < 22 else pc1
                tr(pc[:, c % 22, :], cst[:, c * 128:(c + 1) * 128], identf[0:12, 0:12], [CS, CONST], [CS])
            cp("dve", cvec[:, 0:22, :], pc0[:], [CS], [CONST])
            cp("dve", cvec[:, 22:44, :], pc1[:], [CS], [CONST])
            wtmp = [sb(ph, "wtmp%d" % i, [128, 5632], BF16) for i in range(2)]
            WT = [Buf() for _ in range(2)]
            for c in range(8):
                dma("pool", wtmp[c % 2][:], w_up[c * 128:(c + 1) * 128, :], [], [WT[c % 2]], "wtl%d" % (c % 2))
                dma("sp", wups[c], wtmp[c % 2][:], [WT[c % 2]], [], "wts%d" % (c % 2))
            P.flush()

        with ExitStack() as st_ab:
            obT = sb(st_ab, "obT", [128, 4, NOWN * 128], BF16)
            NAT = [Buf("naT%d" % i) for i in range(NOWN)]
            OBT = Buf("obT")
            SSB = Buf("ssb")
            with ExitStack() as st_q:
                qT = sb(st_q, "qT", [128, 4, NOWN * 128], BF16)
                QT = [Buf("qT%d" % i) for i in range(NOWN)]
                kTn = sb(st_q, "kTn", [128, 4, 128], BF16)
                vnew = sb(st_q, "vnew", [128, 512], BF16)
                KTN = Buf("kTn")
                VNEW = Buf("vnew")
                KTS = Buf("kts")
                VSS = Buf("vss")

                if LEVEL < 1:
                    return nc
                with ExitStack() as ph:
                    win = sb(ph, "win", [128, 8, 2560], BF16)
                    WIN = Buf("win")
                    for c in range(8):
                        dma("pool", win[:, c, :], w_in[c * 128:(c + 1) * 128, :], [], [WIN], "win")
                    xt = [sb(ph, "xt%d" % i, [128, 1024], F32) for i in range(2)]
                    XT = [Buf() for _ in range(2)]
                    junk = sb(ph, "junk", [128, 1024], BF16)
                    JK = Buf()
                    stL = [sb(ph, "stat%d" % i, [128, 16], F32) for i in range(2)]
                    STL = [Buf() for _ in range(2)]
                    hnL = [sb(ph, "hn%d" % i, [128, 1024], BF16) for i in range(2)]
                    HNL = [Buf() for _ in range(2)]
                    xT = [sb(ph, "xT%d" % i, [128, 8, 128], BF16) for i in range(2)]
                    XTT = [Buf() for _ in range(2)]
                    kst = [sb(ph, "kst%d" % i, [128, 4, 128], BF16) for i in range(2)]
                    KST = [Buf() for _ in range(2)]
                    vst = [sb(ph, "vst%d" % i, [128, 512], BF16) for i in range(2)]
                    VST = [Buf() for _ in range(2)]
                    vof = [sb(ph, "vof%d" % i, [128, 512], F32) for i in range(2)]
                    VOF = [Buf() for _ in range(2)]
                    kof = [sb(ph, "kof%d" % i, [128, 512], F32) for i in range(2)]
                    KOF = [Buf() for _ in range(2)]
                    zaL = [sb(ph, "za%d" % i, [128, 1024], F32) for i in range(2)]
                    ZAL = [Buf() for _ in range(2)]
                    vnL = [sb(ph, "vn%d" % i, [128, 512], F32) for i in range(2)]
                    VNL = [Buf() for _ in range(2)]
                    va = [sb(ph, "va%d" % i, [128, 512], F32) for i in range(2)]
                    VA = [Buf() for _ in range(2)]
                    vabL = [sb(ph, "vab%d" % i, [128, 512], BF16) for i in range(2)]
                    VABL = [Buf() for _ in range(2)]
                    outaL = [sb(ph, "outa%d" % i, [128, 512], F32) for i in range(2)]
                    OAL = [Buf() for _ in range(2)]
                    nabL = [sb(ph, "nab%d" % i, [128, 512], BF16) for i in range(2)]
                    NABL = [Buf() for _ in range(2)]
                    nast = [sb(ph, "nast%d" % i, [128, 4, 128], BF16) for i in range(2)]
                    NAST = [Buf() for _ in range(2)]
                    ptr = ps(ph, "ptr", [128, 8, 128], BF16)
                    PTR = Buf()
                    pk = ps(ph, "pk", [128, 512])
                    PK = Buf()
                    pv = ps(ph, "pv", [128, 512])
                    PV = Buf()
                    pk2 = ps(ph, "pk2", [128, 512])
                    PK2 = Buf()
                    pq = ps(ph, "pq", [128, 512])
                    PQ = Buf()
                    puv = ps(ph, "puv", [128, 1024])
                    PUV = Buf()
                    psg = ps(ph, "psg", [128, 512])
                    PSG = Buf()

                    tiles = [("key", p, None) for p in range(HALO)]
                    tiles += [("own", p, p - HALO) for p in range(HALO, NPOS)]
                    tiles += [("samp", None, SAMP)]
                    tiles = [t_ for t_ in tiles if t_[0] in AKINDS]
                    for ti, (kind, pos, oi) in enumerate(tiles):
                        s = ti % 2
                        st, ST, hn, HN, za, ZA, vn, VN = stL[s], STL[s], hnL[s], HNL[s], zaL[s], ZAL[s], vnL[s], VNL[s]
                        vab, VAB, outa, OA, nab, NAB = vabL[s], VABL[s], outaL[s], OAL[s], nabL[s], NABL[s]
                        src = xs if kind == "samp" else xk[pos * 128:(pos + 1) * 128, :]
                        dma("sp", xt[s][:], src, [], [XT[s]], "xt%d" % s)
                        act(junk[:], xt[s][:], AF.Square, [XT[s]], [ST], accum_out=st[:, 0:1])
                        rstd(st[:, 1:2], st[:, 0:1], 1024, ST)
                        stt("dve", hn[:], xt[s][:], st[:, 1:2], gpre[:], ALU.mult, ALU.mult, [XT[s], ST, CONST], [HN])
                        for c in range(8):
                            tr(ptr[:, c, :], hn[:, c * 128:(c + 1) * 128], ident[:], [HN, CONST], [PTR])
                        cp("dve", xT[s][:], ptr[:], [PTR], [XTT[s]])
                        for fc in range(4):
                            for c in range(8):
                                mm(pk[:, fc * 128:(fc + 1) * 128], win[:, c, 1536 + fc * 128:1536 + (fc + 1) * 128], xT[s][:, c, :],
                                   fc == 0 and c == 0, fc == 3 and c == 7, [WIN, XTT[s]], [PK])
                        for c in range(8):
                            mm(pv[:], xT[s][:, c, :], win[:, c, 2048:2560], c == 0, c == 7, [WIN, XTT[s]], [PV])
                        if kind == "samp":
                            act(kTn[:].rearrange("p c t -> p (c t)"), pk[:], AF.Copy, [PK], [KTN], scale=0.125)
                        else:
                            act(kst[s][:].rearrange("p c t -> p (c t)"), pk[:], AF.Copy, [PK], [KST[s]], scale=0.125)
                            dma("pool", kts.rearrange("c p t -> p c t")[:, :, pos * 128:(pos + 1) * 128], kst[s][:], [KST[s]], [], "kst%d" % s)
                        if kind == "key":
                            cp("dve", vst[s][:], pv[:], [PV], [VST[s]])
                        else:
                            act(vof[s][:], pv[:], AF.Copy, [PV], [VOF[s]])
                            cp("dve", vst[s][:], vof[s][:], [VOF[s]], [VST[s]])
                        if kind == "samp":
                            cp("dve", vnew[:], vof[s][:], [VOF[s]], [VNEW])
                        else:
                            dma("pool", vss.rearrange("c p n d -> p c n d")[:, :, pos, :], vst[s][:].rearrange("p (c d) -> p c d", d=128), [VST[s]], [], "vst%d" % s)
                        if kind == "key":
                            continue
                        if ASTOP < 1:
                            continue
                        for c in range(8):
                            mm(pk2[:], xT[s][:, c, :], win[:, c, 1536:2048], c == 0, c == 7, [WIN, XTT[s]], [PK2])
                        act(kof[s][:], pk2[:], AF.Copy, [PK2], [KOF[s]])
                        if kind == "samp":
                            dma("pool", vso_o, vof[s][:], [VOF[s]], [], "vof%d" % s)
                            dma("pool", kso_o, kof[s][:], [KOF[s]], [], "kof%d" % s)
                        elif oi >= 1:
                            dma("pool", vo_o[(oi - 1) * 128:oi * 128, :], vof[s][:], [VOF[s]], [], "vof%d" % s)
                            dma("pool", ko_o[(oi - 1) * 128:oi * 128, :], kof[s][:], [KOF[s]], [], "kof%d" % s)
                        if ASTOP < 2:
                            continue
                        for fc in range(4):
                            for c in range(8):
                                mm(pq[:, fc * 128:(fc + 1) * 128], win[:, c, 1024 + fc * 128:1024 + (fc + 1) * 128], xT[s][:, c, :],
                                   fc == 0 and c == 0, fc == 3 and c == 7, [WIN, XTT[s]], [PQ])
                        cp("dve", qT[:, :, oi * 128:(oi + 1) * 128], pq[:].rearrange("p (c t) -> p c t", t=128), [PQ], [QT[oi]])
                        if ASTOP < 3:
                            continue
                        for hf in range(2):
                            for c in range(8):
                                mm(puv[:, hf * 512:(hf + 1) * 512], xT[s][:, c, :], win[:, c, hf * 512:(hf + 1) * 512], c == 0, c == 7, [WIN, XTT[s]], [PUV])
                        act(za[:], puv[:], AF.Gelu_apprx_tanh, [PUV], [ZA])
                        if ASTOP < 4:
                            continue
                        act(junk[:, 0:512], za[:, 512:1024], AF.Identity, [ZA], [ST], accum_out=st[:, 2:3])
                        act(junk[:, 512:1024], za[:, 512:1024], AF.Square, [ZA], [ST], accum_out=st[:, 3:4])
                        ts("dve", st[:, 4:5], st[:, 2:3], 1.0 / 512, None, ALU.mult, ALU.bypass, [ST], [ST])
                        tt("dve", st[:, 5:6], st[:, 4:5], st[:, 4:5], ALU.mult, [ST], [ST])
                        stt("dve", st[:, 6:7], st[:, 3:4], 1.0 / 512, st[:, 5:6], ALU.mult, ALU.subtract, [ST], [ST])
                        ts("dve", st[:, 6:7], st[:, 6:7], 1.0, EPS, ALU.mult, ALU.add, [ST], [ST])
                        act(st[:, 6:7], st[:, 6:7], AF.Sqrt, [ST], [ST])
                        add("dve", lambda e, st=st: e.reciprocal(out=st[:, 7:8], in_=st[:, 6:7]), [ST], [ST])
                        stt("dve", st[:, 8:9], st[:, 4:5], -1.0, st[:, 7:8], ALU.mult, ALU.mult, [ST], [ST])
                        act(vn[:], za[:, 512:1024], AF.Identity, [ZA, ST], [VN], scale=st[:, 7:8], bias=st[:, 8:9])
                        tt("dve", vn[:], vn[:], lng[:], ALU.mult, [VN, CONST], [VN])
                        tt("dve", va[s][:], vn[:], lnb[:], ALU.add, [VN, CONST], [VA[s]])
                        if kind == "samp":
                            dma("pool", vas_o, va[s][:], [VA[s]], [], "va%d" % s)
                        cp("pool", vab[:], va[s][:], [VA[s]], [VAB])
                        if ASTOP < 5:
                            continue
                        w_ = wsTs if kind == "samp" else wsT
                        b_ = bsps if kind == "samp" else bsp
                        for h in range(4):
                            mm(psg[:, h * 128:(h + 1) * 128], w_[:, h, :], vab[:, h * 128:(h + 1) * 128], h == 0, h == 3, [CONST, VAB], [PSG])
                        for h in range(4):
                            stt("dve", outa[:, h * 128:(h + 1) * 128], psg[:, h * 128:(h + 1) * 128], b_[:, h:h + 1], za[:, h * 128:(h + 1) * 128],
                                ALU.add, ALU.mult, [PSG, CONST, ZA], [OA])
                        if ASTOP < 6:
                            continue
                        act(junk[:, 0:512], outa[:], AF.Square, [OA], [ST], accum_out=st[:, 9:10])
                        rstd(st[:, 10:11], st[:, 9:10], 512, ST)
                        stt("dve", nab[:], outa[:], st[:, 10:11], goa[:], ALU.mult, ALU.mult, [OA, ST, CONST], [NAB])
                        for c in range(4):
                            tr(ptr[:, c, :], nab[:, c * 128:(c + 1) * 128], ident[:], [NAB, CONST], [PTR])
                        cp("dve", nast[s][:], ptr[:, 0:4, :], [PTR], [NAST[s]])
                        dma("pool", nas.rearrange("c p t -> p c t")[:, :, oi * 128:(oi + 1) * 128], nast[s][:], [NAST[s]], [], "nast%d" % s)
                    P.flush()

                if LEVEL < 2:
                    return nc
                with ExitStack() as ph:
                    ebuf = [sb(ph, "e%d" % i, [128, 2, 512], F32) for i in range(2)]
                    EB = [Buf() for _ in range(2)]
                    spb = [sb(ph, "sp%d" % i, [128, 2, 512], BF16) for i in range(2)]
                    SPB = [Buf() for _ in range(2)]
                    wtb = [sb(ph, "wt%d" % i, [128, 2, 512], BF16) for i in range(2)]
                    WTB = [Buf() for _ in range(2)]
                    srun = sb(ph, "srun", [128, 2, 512], BF16)
                    SR = Buf()
                    sqf = sb(ph, "sqf", [128, 512], BF16)
                    SQF = Buf()
                    pz = [ps(ph, "pz%d" % i, [128, 2, 512]) for i in range(2)]
                    PZ = [Buf() for _ in range(2)]
                    pa = ps(ph, "pa", [128, 2, 512])
                    PA = Buf()
                    po = ps(ph, "po", [128, 512])
                    PO = Buf()
                    pss = ps(ph, "pss", [128, NOWN, 2])
                    add("dve", lambda e: e.memset(pss[:], 0.0), [], [SSB])

                    def attention(steps, finish):
                        n = len(steps)

                        def zmm(dst, DST, stp):
                            banks = set()
                            for sg in stp["segs"]():
                                mm(sg["zout"](dst), sg["kT"], sg["q"], sg["bank"] not in banks, False, stp["reads"], [DST])
                                banks.add(sg["bank"])
                                if sg.get("mask") is not None:
                                    mm(sg["mout"](dst), sg["mlhs"], sg["mask"], False, False, [CONST], [DST])

                        def view(t, stp):
                            return stp["view"](t)

                        def pvmm(si, last):
                            pvs = steps[si]
                            if SKIPPV:
                                return
                            for (o_, l_, r_, first, tp) in pvs["pv"](po, wtb[si % 2]):
                                mm(o_, l_, r_, si == 0 and first, last, pvs["reads"] + [WTB[si % 2]], [PO], tile_position=tp)

                        add("dve", lambda e: e.memset(srun[:], 0.0), [], [SR])
                        zmm(pz[0], PZ[0], steps[0])
                        act(view(ebuf[0], steps[0]), view(pz[0], steps[0]), AF.Exp, [PZ[0]], [EB[0]])
                        act(view(spb[0], steps[0]), view(ebuf[0], steps[0]), AF.Ln, [EB[0]], [SPB[0]], bias=1.0)
                        for s_ in range(n):
                            b = s_ % 2
                            nb = (s_ + 1) % 2
                            stp = steps[s_]
                            if s_ + 1 < n:
                                nx = steps[s_ + 1]
                                zmm(pz[nb], PZ[nb], nx)
                                act(view(ebuf[nb], nx), view(pz[nb], nx), AF.Exp, [PZ[nb]], [EB[nb]])
                            zmm(pa, PA, stp)
                            if s_ > 0:
                                for (o_, l_, r_) in stp["carry"](pa, srun):
                                    mm(o_, l_, r_, False, False, [CONST, SR], [PA])
                            for (o_, l_, r_) in stp["cum"](pa, spb[b]):
                                mm(o_, l_, r_, False, False, [CONST, SPB[b]], [PA])
                            act(view(wtb[b], stp), view(pa, stp), AF.Exp, [PA], [WTB[b]])
                            if s_ + 1 < n:
                                act(view(spb[nb], nx), view(ebuf[nb], nx), AF.Ln, [EB[nb]], [SPB[nb]], bias=1.0)
                            tt("dve", view(srun, stp), view(srun, stp), view(spb[b], stp), ALU.add, [SR, SPB[b]], [SR])
                            if s_ > 0:
                                pvmm(s_ - 1, False)
                        pvmm(n - 1, True)
                        if SKIPPV < 2:
                            finish()

                    with ExitStack() as ph0:
                        ckb = sb(ph0, "ckb", [128, 8, 512], BF16)
                        CKB = Buf()
                        kTs = sb(ph0, "kTs", [128, 4, 2048], BF16)
                        KTSB = Buf()
                        vsb = sb(ph0, "vsb", [128, 16, 512], BF16)
                        VSB = Buf()
                        sqs = sb(ph0, "sqs", [128, 4, 128], BF16)
                        SQS = Buf()
                        ptk = po[:].bitcast(BF16).rearrange("p (j t) -> p j t", t=128)
                        for bi in range(4):
                            for half in range(2):
                                dma("pool", vsb[:, half * 8:(half + 1) * 8, :], cv[bi, half * 1024:(half + 1) * 1024, :].rearrange("(n p) f -> p n f", p=128), [], [VSB], "vsb")
                            for half in range(2):
                                dma("pool", ckb[:], ck[bi, half * 1024:(half + 1) * 1024, :].rearrange("(n p) f -> p n f", p=128), [], [CKB], "ckb")
                                for hp in range(4):
                                    for j in range(8):
                                        tr(ptk[:, j, :], ckb[:, j, hp * 128:(hp + 1) * 128], ident[:], [CKB, CONST], [PO])
                                    act(kTs[:, hp, half * 1024:(half + 1) * 1024], ptk.rearrange("p j t -> p (j t)"), AF.Copy, [PO], [KTSB], scale=0.125)
                            qc = SAMP * 128 + 32 * bi
                            steps = []
                            for kp in [16] + list(range(15, -1, -1)):
                                new = kp == 16
                                KPp = 128

                                def segs(new=new, kp=kp, KPp=KPp, qc=qc, bi=bi):
                                    out = []
                                    for h in range(8):
                                        hp, par = h // 2, h % 2
                                        r0 = 64 * par
                                        if new:
                                            kT_ = kTn[r0:r0 + 64, hp, :]
                                        else:
                                            kT_ = kTs[r0:r0 + 64, hp, kp * 128:(kp + 1) * 128]
                                        sg = {"kT": kT_, "q": qT[r0:r0 + 64, hp, qc:qc + 32], "bank": par,
                                              "zout": (lambda d, hp=hp, par=par: d[:, par, hp * 32:(hp + 1) * 32])}
                                        if new and h >= 6:
                                            sg["mask"] = masks[:, bi, 0:4].rearrange("p h q -> p (h q)")
                                            sg["mlhs"] = ident[:]
                                            sg["mout"] = (lambda d, par=par: d[:, par, 0:128])
                                        out.append(sg)
                                    return out

                                def pvf(po_, wt_, new=new, kp=kp, KPp=KPp, bi=bi):
                                    out = []
                                    for h in range(8):
                                        hp, par = h // 2, h % 2
                                        l_ = vnew[:, h * 64:(h + 1) * 64] if new else vsb[:, kp, h * 64:(h + 1) * 64]
                                        out.append((po_[64 * par:64 * par + 64, hp * 32:(hp + 1) * 32], l_, wt_[:, par, hp * 32:(hp + 1) * 32],
                                                    h < 2, (0, 64 * par)))
                                    return out

                                steps.append({
                                    "reads": [KTN, VNEW, QT[SAMP]] if new else [KTSB, VSB, QT[SAMP]],
                                    "view": (lambda t: t[:, :, 0:128]),
                                    "cum": (lambda pa_, sp_: [(pa_[:, par, 0:128], negT[:], sp_[:, par, 0:128]) for par in range(2)]),
                                    "carry": (lambda pa_, sr_: [(pa_[:, par, 0:128], negO[:], sr_[:, par, 0:128]) for par in range(2)]),
                                    "pv": pvf,
                                    "segs": segs,
                                })

                            def fin(bi=bi, qc=qc):
                                act(sqs[:, :, 32 * bi:32 * bi + 32], po[:, 0:128].rearrange("p (c q) -> p c q", q=32), AF.Square, [PO], [SQS])
                                for hp in range(4):
                                    act(obT[:, hp, qc:qc + 32], po[:, hp * 32:(hp + 1) * 32], AF.Identity, [PO, CONST], [OBT], scale=gob[:, hp:hp + 1])

                            if B0MODE >= 1:
                                attention(steps[:B0MODE], fin)
                        for hp in range(4):
                            mm(pss[:, SAMP, :], sqs[:, hp, :], onesf[:, 0:2], False, False, [SQS, CONST], [SSB])
                        P.flush()

                    if LEVEL < 3:
                        return nc
                    kTp = sb(ph, "kTp", [128, NPOS * 128], BF16)
                    KTP = Buf()
                    vp = sb(ph, "vp", [128, NPOS, 128], BF16)
                    VP = Buf()
                    groups = [[0]] + [[1 + 4 * g + j for j in range(4)] for g in range(NPR // 4)]
                    for hp in range(4):
                        dma("sp", kTp[:], kts[hp], [], [KTP], "kTp")
                        dma("sp", vp[:], vss[hp], [], [VP], "vp")
                        for grp in groups:
                            n = len(grp)
                            W = n * 128
                            p0 = HALO + grp[0]
                            qc = grp[0] * 128
                            steps = []
                            for kp in range(p0 + n - 1, -1, -1):
                                c0 = (kp - p0) * 128 if kp >= p0 else 0
                                diag = kp >= p0

                                def segs(kp=kp, c0=c0, diag=diag, W=W, qc=qc, hp=hp):
                                    out = []
                                    for par in range(2):
                                        r0 = 64 * par
                                        sg = {"kT": kTp[r0:r0 + 64, kp * 128:(kp + 1) * 128], "q": qT[r0:r0 + 64, hp, qc + c0:qc + W], "bank": par,
                                              "zout": (lambda d, par=par, c0=c0, W=W: d[:, par, c0:W])}
                                        if diag:
                                            sg["mask"] = maskb[:]
                                            sg["mlhs"] = ident[:]
                                            sg["mout"] = (lambda d, par=par, c0=c0: d[:, par, c0:c0 + 128])
                                        out.append(sg)
                                    return out

                                def pvf(po_, wt_, kp=kp, c0=c0, W=W):
                                    return [(po_[64 * par:64 * par + 64, c0:W], vp[:, kp, 64 * par:64 * par + 64], wt_[:, par, c0:W],
                                             True, (0, 64 * par)) for par in range(2)]

                                steps.append({
                                    "reads": [KTP, VP] + [QT[o] for o in grp],
                                    "view": (lambda t, c0=c0, W=W: t[:, :, c0:W]),
                                    "cum": (lambda pa_, sp_, c0=c0, W=W: [(pa_[:, par, c0:W], negT[:], sp_[:, par, c0:W]) for par in range(2)]),
                                    "carry": (lambda pa_, sr_, c0=c0, W=W: [(pa_[:, par, c0:W], negO[:], sr_[:, par, c0:W]) for par in range(2)]),
                                    "pv": pvf,
                                    "segs": segs,
                                })

                            def fin(grp=grp, W=W, qc=qc, hp=hp):
                                act(sqf[:, 0:W], po[:, 0:W], AF.Square, [PO], [SQF])
                                act(obT[:, hp, qc:qc + W], po[:, 0:W], AF.Identity, [PO, CONST], [OBT], scale=gob[:, hp:hp + 1])
                                for j, o in enumerate(grp):
                                    mm(pss[:, o, :], sqf[:, j * 128:(j + 1) * 128], onesf[:, 0:2], False, False, [SQF, CONST], [SSB])

                            attention(steps, fin)
                    cp("dve", ssb[:], pss[:, :, 0], [SSB], [SSB])
                    P.flush()
            if LEVEL < 4:
                return nc
            X1S = Buf("x1s")
            H2S = Buf("h2s")
            with ExitStack() as ph:
                wo = sb(ph, "wo", [128, 8, 1024], BF16)
                WO = Buf()
                for c in range(8):
                    dma("pool", wo[:, c, :], w_out[c * 128:(c + 1) * 128, :], [], [WO], "wo")
                nat = [sb(ph, "nat%d" % i, [128, 4, 128], BF16) for i in range(2)]
                NATB = [Buf() for _ in range(2)]
                xt = [sb(ph, "cxt%d" % i, [128, 1024], F32) for i in range(2)]
                XT = [Buf() for _ in range(2)]
                mixaL = [sb(ph, "mixa%d" % i, [128, 1024], F32) for i in range(2)]
                MAL = [Buf() for _ in range(2)]
                mixL = [sb(ph, "mix%d" % i, [128, 1024], F32) for i in range(2)]
                MXL = [Buf() for _ in range(2)]
                x1 = [sb(ph, "x1_%d" % i, [128, 1024], F32) for i in range(2)]
                X1 = [Buf() for _ in range(2)]
                junk = sb(ph, "cjunk", [128, 1024], BF16)
                JK = Buf()
                stL = [sb(ph, "cstat%d" % i, [128, 8], F32) for i in range(2)]
                STL = [Buf() for _ in range(2)]
                hn2L = [sb(ph, "hn2_%d" % i, [128, 1024], BF16) for i in range(2)]
                HN2L = [Buf() for _ in range(2)]
                h2st = [sb(ph, "h2st%d" % i, [128, 8, 128], BF16) for i in range(2)]
                H2ST = [Buf() for _ in range(2)]
                pma = ps(ph, "pma", [128, 1024])
                PMA = Buf()
                pmb = ps(ph, "pmb", [128, 1024])
                PMB = Buf()
                ptr = ps(ph, "cptr", [128, 8, 128], BF16)
                PTR = Buf()
                for oi in range(NOWN):
                    s = oi % 2
                    mixa, MA, mix, MX, st, ST, hn2, HN2 = mixaL[s], MAL[s], mixL[s], MXL[s], stL[s], STL[s], hn2L[s], HN2L[s]
                    src = xs if oi == SAMP else xk[(HALO + oi) * 128:(HALO + oi + 1) * 128, :]
                    dma("sp", xt[s][:], src, [], [XT[s]], "cxt%d" % s)
                    cols = slice(oi * 128, (oi + 1) * 128)
                    dma("sp", nat[s][:], nas.rearrange("c p t -> p c t")[:, :, cols], [], [NATB[s]], "nat%d" % s)
                    for hf in range(2):
                        for c in range(4):
                            mm(pma[:, hf * 512:(hf + 1) * 512], nat[s][:, c, :], wo[:, c, hf * 512:(hf + 1) * 512], c == 0, c == 3, [NATB[s], WO], [PMA])
                    for hf in range(2):
                        for c in range(4):
                            mm(pmb[:, hf * 512:(hf + 1) * 512], obT[:, c, cols], wo[:, 4 + c, hf * 512:(hf + 1) * 512], c == 0, c == 3, [OBT, WO], [PMB])
                    rstd(st[:, 0:1], ssb[:, oi:oi + 1], 512, ST if oi else ST)
                    act(mixa[:], pma[:], AF.Copy, [PMA], [MA])
                    stt("dve", mix[:], pmb[:], st[:, 0:1], mixa[:], ALU.mult, ALU.add, [PMB, ST, MA, SSB], [MX])
                    act(junk[:], mix[:], AF.Square, [MX], [ST], accum_out=st[:, 1:2])
                    rstd(st[:, 2:3], st[:, 1:2], 1024, ST)
                    stt("dve", mix[:], mix[:], st[:, 2:3], gpost[:], ALU.mult, ALU.mult, [MX, ST, CONST], [MX])
                    tt("dve", x1[s][:], mix[:], xt[s][:], ALU.add, [MX, XT[s]], [X1[s]])
                    dma("pool", x1s[oi], x1[s][:], [X1[s]], [], "x1st%d" % s)
                    act(junk[:], x1[s][:], AF.Square, [X1[s]], [ST], accum_out=st[:, 3:4])
                    rstd(st[:, 4:5], st[:, 3:4], 1024, ST)
                    stt("dve", hn2[:], x1[s][:], st[:, 4:5], gpf[:], ALU.mult, ALU.mult, [X1[s], ST, CONST], [HN2])
                    for c in range(8):
                        tr(ptr[:, c, :], hn2[:, c * 128:(c + 1) * 128], ident[:], [HN2, CONST], [PTR])
                    cp("dve", h2st[s][:], ptr[:], [PTR], [H2ST[s]])
                    dma("pool", h2s.rearrange("c p t -> p c t")[:, :, oi * 128:(oi + 1) * 128], h2st[s][:], [H2ST[s]], [], "h2st%d" % s)
                P.flush()
        if LEVEL < 5:
            return nc
        with ExitStack() as ph:
            wd = sb(ph, "wd", [128, 22, 1024], BF16)
            WD = Buf()
            for c in range(22):
                dma("pool", wd[:, c, :], w_down[c * 128:(c + 1) * 128, :], [], [WD], "wd")
            hg = [sb(ph, "hg%d" % i, [128, 8, 512], BF16) for i in range(2)]
            HG = [Buf() for _ in range(2)]
            wuf = [sb(ph, "wuf%d" % i, [128, 2, 8, 128], BF16) for i in range(3)]
            WUF = [Buf() for _ in range(3)]
            ug = [sb(ph, "ug%d" % i, [128, 514], F32) for i in range(2)]
            UG = [Buf() for _ in range(2)]
            uv = [sb(ph, "uv%d" % i, [128, 514], F32) for i in range(2)]
            UV = [Buf() for _ in range(2)]
            yg = [sb(ph, "yg%d" % i, [128, 512], F32) for i in range(2)]
            YG = [Buf() for _ in range(2)]
            yv = [sb(ph, "yv%d" % i, [128, 512], F32) for i in range(2)]
            YV = [Buf() for _ in range(2)]
            gl = [sb(ph, "gl%d" % i, [128, 512], F32) for i in range(2)]
            GL = [Buf() for _ in range(2)]
            aT = sb(ph, "aT", [128, 22, 512], BF16)
            AT = Buf()
            HIST = Buf()
            x1t = [sb(ph, "x1t%d" % i, [128, 1024], F32) for i in range(2)]
            X1T = [Buf() for _ in range(2)]
            yo = [sb(ph, "yo%d" % i, [128, 1024], F32) for i in range(2)]
            YO = [Buf() for _ in range(2)]
            tmst = [sb(ph, "tmst%d" % i, [128, 256], F32) for i in range(2)]
            TMST = [Buf() for _ in range(2)]
            junk = sb(ph, "djunk", [128, 1024], BF16)
            JK = Buf()
            st = sb(ph, "dstat", [128, 4], F32)
            ST = Buf()
            pg = [ps(ph, "pg%d" % i, [128, 512]) for i in range(2)]
            PG = [Buf() for _ in range(2)]
            pvv = [ps(ph, "pvv%d" % i, [128, 512]) for i in range(2)]
            PVV = [Buf() for _ in range(2)]
            pf = ps(ph, "pf", [128, 1024])
            PF = Buf()
            pt = ps(ph, "pt", [128, 256])
            PT = Buf()
            h2v = h2s.rearrange("c p t -> p c t")
            cnt = {"g": 0, "f": 0, "t": 0, "x": 0}

            def load_wuf(fc):
                s = cnt["f"] % 3
                cnt["f"] += 1
                wv = wups.rearrange("c p n -> p c n")
                dma("sp", wuf[s][:, 0], wv[:, :, fc * 128:(fc + 1) * 128], [], [WUF[s]], "wuf%d" % s)
                dma("sp", wuf[s][:, 1], wv[:, :, (22 + fc) * 128:(23 + fc) * 128], [], [WUF[s]], "wuf%d" % s)
                return s

            gs = cnt["g"] % 2
            cnt["g"] += 1
            dma("sp", hg[gs][:, :, 0:2], h2v[:, :, 126:128], [], [HG[gs]], "hg%d" % gs)
            for fc in range(22):
                ws = load_wuf(fc)
                for typ in range(2):
                    ch = fc + 22 * typ
                    for c in range(8):
                        mm(pg[0][:, ch * 2:ch * 2 + 2], wuf[ws][:, typ, c, :], hg[gs][:, c, 0:2], fc == 0 and typ == 0 and c == 0, fc == 21 and typ == 1 and c == 7,
                           [WUF[ws], HG[gs]], [PG[0]])
            cp("dve", hist[:], pg[0][:, 0:88].rearrange("p (f t) -> p f t", t=2), [PG[0]], [HIST])

            def ffn_group(ois, sample, tm_tile, out_fn):
                n = len(ois)
                W = n * 128
                gs = cnt["g"] % 2
                cnt["g"] += 1
                dma("sp", hg[gs][:, :, 0:W], h2v[:, :, ois[0] * 128:ois[0] * 128 + W], [], [HG[gs]], "hg%d" % gs)
                for fc in range(22):
                    ws = load_wuf(fc)
                    b = fc % 2
                    for typ, (pp, PP, uu, UU, yy, YY) in enumerate([(pg[b], PG[b], ug[b], UG[b], yg[b], YG[b]), (pvv[b], PVV[b], uv[b], UV[b], yv[b], YV[b])]):
                        ch = fc + 22 * typ
                        for c in range(8):
                            mm(pp[:, 0:W], wuf[ws][:, typ, c, :], hg[gs][:, c, 0:W], c == 0, c == 7, [WUF[ws], HG[gs]], [PP])
                        act(uu[:, 2:2 + W], pp[:, 0:W], AF.Copy, [PP], [UU])
                        act(yy[:, 0:W], pp[:, 0:W], AF.Identity, [PP, CONST], [YY], scale=cvec[:, ch, 2:3], bias=cvec[:, ch, 3:4])
                        if sample:
                            cp("pool", uu[:, 0:128].rearrange("p (i c) -> p i c", c=32)[:, :, 0:2], cvec[:, ch, 4:12].rearrange("p (i c) -> p i c", c=2), [CONST, UU], [UU])
                        else:
                            cp("pool", uu[:, 0:2], hist[:, ch, :], [HIST, UU], [UU])
                            cp("pool", hist[:, ch, :], uu[:, W:W + 2], [UU, HIST], [HIST])
                        stt("dve", yy[:, 0:W], uu[:, 1:1 + W], cvec[:, ch, 1:2], yy[:, 0:W], ALU.mult, ALU.add, [UU, CONST, YY], [YY])
                        stt("dve", yy[:, 0:W], uu[:, 0:W], cvec[:, ch, 0:1], yy[:, 0:W], ALU.mult, ALU.add, [UU, CONST, YY], [YY])
                    act(gl[b][:, 0:W], yg[b][:, 0:W], AF.Gelu_apprx_tanh, [YG[b]], [GL[b]])
                    tt("pool", aT[:, fc, 0:W], gl[b][:, 0:W], yv[b][:, 0:W], ALU.mult, [GL[b], YV[b]], [AT])
                    if tm_tile is not None:
                        j, dst = tm_tile
                        ts_ = cnt["t"] % 2
                        cnt["t"] += 1
                        for typ in range(2):
                            for c in range(8):
                                mm(pt[:, typ * 128:(typ + 1) * 128], hg[gs][:, c, j * 128:(j + 1) * 128], wuf[ws][:, typ, c, :],
                                   typ == 0 and c == 0, typ == 1 and c == 7, [WUF[ws], HG[gs]], [PT])
                        act(tmst[ts_][:], pt[:], AF.Copy, [PT], [TMST[ts_]])
                        dma("pool", dst[:, fc * 128:(fc + 1) * 128], tmst[ts_][:, 0:128], [TMST[ts_]], [], "tmst%d" % ts_)
                        dma("pool", dst[:, (22 + fc) * 128:(23 + fc) * 128], tmst[ts_][:, 128:256], [TMST[ts_]], [], "tmst%d" % ts_)
                for j, oi in enumerate(ois):
                    xs_ = cnt["x"] % 2
                    cnt["x"] += 1
                    dma("sp", x1t[xs_][:], x1s[oi], [], [X1T[xs_]], "x1t%d" % xs_)
                    for hf in range(2):
                        for fc in range(22):
                            mm(pf[:, hf * 512:(hf + 1) * 512], aT[:, fc, j * 128:(j + 1) * 128], wd[:, fc, hf * 512:(hf + 1) * 512], fc == 0, fc == 21, [AT, WD], [PF])
                    act(junk[:], pf[:], AF.Square, [PF], [ST], accum_out=st[:, 0:1])
                    rstd(st[:, 1:2], st[:, 0:1], 1024, ST)
                    stt("dve", yo[xs_][:], pf[:], st[:, 1:2], gpostf[:], ALU.mult, ALU.mult, [PF, ST, CONST], [YO[xs_]])
                    tt("dve", yo[xs_][:], yo[xs_][:], x1t[xs_][:], ALU.add, [YO[xs_], X1T[xs_]], [YO[xs_]])
                    dma("pool", out_fn(oi), yo[xs_][:], [YO[xs_]], [], "yo%d" % xs_)

            for g in range(NPR // 4):
                ois = [1 + 4 * g + j for j in range(4)]
                ffn_group(ois, False, (3, upl_o) if g == NPR // 4 - 1 else None, lambda oi: y_o[(oi - 1) * 128:oi * 128, :])
            ffn_group([SAMP], True, (0, ups_o), lambda oi: ys_o)
            P.flush()
      except _Stop:
        pass
    return nc


_NC = None


def kernel(**inputs):
    global _NC
    f = lambda k: np.ascontiguousarray(np.asarray(inputs[k], dtype=np.float32))
    xp, xsm = f("x_prompt"), f("x_sample")
    ckf = f("cache_sb_k")[0].reshape(32, 2048, 512)
    cvf = f("cache_sb_v")[0].reshape(32, 2048, 512)
    ccf = f("cache_ffn_conv")[0]
    shared = {
        "w_in": f("w_in")[0], "w_out": f("w_out")[0], "w_up": f("w_up")[0], "w_down": f("w_down")[0],
        "g_pre_mix": f("g_pre_mix")[0], "ln_v_g": f("ln_v_g")[0], "ln_v_b": f("ln_v_b")[0],
        "w_spatial": f("w_spatial")[0], "b_spatial": f("b_spatial")[0], "g_out_a": f("g_out_a")[0],
        "g_out_b": f("g_out_b")[0], "g_post_mix": f("g_post_mix")[0], "g_pre_ffn": f("g_pre_ffn")[0],
        "conv_w": f("conv_w")[0], "conv_b": f("conv_b"), "g_post_ffn": f("g_post_ffn")[0],
    }
    in_maps = []
    for c in range(8):
        b, half = c // 2, c % 2
        xk = np.zeros((NPOS * 128, 1024), np.float32)
        if half == 0:
            xk[4096:] = xp[b, 0:4096]
        else:
            xk[:] = xp[b]
        xs = np.zeros((128, 1024), np.float32)
        for i in range(4):
            xs[32 * i:32 * i + 16] = xsm[4 * c + i]
        m = dict(shared)
        m.update({"xk": xk, "xs": xs, "ck": np.ascontiguousarray(ckf[4 * c:4 * c + 4]), "cv": np.ascontiguousarray(cvf[4 * c:4 * c + 4]),
                  "cconv": np.ascontiguousarray(ccf[4 * c:4 * c + 4].reshape(8, 5632))})
        in_maps.append(m)
    if _NC is None:
        _NC = build()
    res = run_bass_kernel_spmd(_NC, in_maps, core_ids=list(range(8)))
    R = res.results
    yp = np.zeros((4, 8192, 1024), np.float32)
    ys = np.zeros((32, 16, 1024), np.float32)
    kp = np.zeros((1, 4, 8192, 8, 64), np.float32)
    vp = np.zeros((1, 4, 8192, 8, 64), np.float32)
    ksn = np.zeros((1, 32, 16, 8, 64), np.float32)
    vsn = np.zeros((1, 32, 16, 8, 64), np.float32)
    vas = np.zeros((1, 32, 16, 512), np.float32)
    cpp = np.zeros((1, 4, 2, 5632), np.float32)
    css = np.zeros((1, 32, 2, 5632), np.float32)
    for c in range(8):
        b, half = c // 2, c % 2
        r = R[c]
        sl = slice(half * 4096, (half + 1) * 4096)
        yp[b, sl] = r["y"]
        kp[0, b, sl] = r["ko"].reshape(4096, 8, 64)
        vp[0, b, sl] = r["vo"].reshape(4096, 8, 64)
        if half == 1:
            cpp[0, b] = r["upl"][126:128]
        for i in range(4):
            rows = slice(32 * i, 32 * i + 16)
            ys[4 * c + i] = r["ys"][rows]
            ksn[0, 4 * c + i] = r["kso"][rows].reshape(16, 8, 64)
            vsn[0, 4 * c + i] = r["vso"][rows].reshape(16, 8, 64)
            vas[0, 4 * c + i] = r["vas"][rows]
            css[0, 4 * c + i] = r["ups"][32 * i + 14:32 * i + 16]
    return (yp, ys, kp, vp, ksn, vsn, vas, cpp, css)
```
